# Optimizing a Trainium2 kernel written in Bass

```python
import math
import jax, jax.numpy as jnp
from jax import lax
import numpy as np

D_MODEL = 1024
BATCH = 8
SEQ = 4096
DEPTH = 1

GRID_W = 64
CTX_LEN = 256
POOL_GROUPS = 4
POOL_WINDOWS = (2, 4, 8, 16)
POOL_WIDTH = D_MODEL
POOL_GROUP_DIM = POOL_WIDTH // POOL_GROUPS
M_HEADS = 4
M_HEAD_DIM = D_MODEL // M_HEADS
M_WIDTH = M_HEADS * M_HEAD_DIM
CONV_W = 3
CHUNK = 128
D_FF = -(-8 * D_MODEL // (3 * 256)) * 256
N_GATE_COLS = 4 * M_HEADS
IN_COLS = POOL_WIDTH + 4 * M_WIDTH + 2 * D_MODEL + N_GATE_COLS
EPS = 1e-6

kernel_name = "hybrid_pool_mlstm_dit_prefix"


def _rmsnorm(x, gain):
    xf = x.astype(jnp.float32)
    y = xf * lax.rsqrt(jnp.mean(xf * xf, axis=-1, keepdims=True) + EPS)
    return y.astype(x.dtype) * gain


def _modulate(x, gain, shift, scale):
    return _rmsnorm(x, gain) * (1 + scale) + shift


def _swiglu(h, w_in, w_out):
    g, u = jnp.split(h @ w_in, 2, axis=-1)
    return (jax.nn.silu(g) * u) @ w_out


def _window_sum(x, w, axis):
    n = x.shape[axis]
    cs = jnp.cumsum(x, axis=axis)
    zero = jnp.zeros_like(lax.slice_in_dim(cs, 0, 1, axis=axis))
    cs = jnp.concatenate([zero, cs], axis=axis)
    t = jnp.arange(n)
    lo = jnp.clip(t - w // 2, 0, n)
    hi = jnp.clip(t + w // 2, 0, n)
    s = jnp.take(cs, hi, axis=axis) - jnp.take(cs, lo, axis=axis)
    return s, (hi - lo).astype(x.dtype)


def _pool_mixer(u, w_pool, scale, on_grid):
    B, N, _ = u.shape
    G = POOL_GROUP_DIM
    uf = u.astype(jnp.float32)
    outs = []
    for g, w in enumerate(POOL_WINDOWS):
        ug = uf[..., g * G:(g + 1) * G]
        if on_grid:
            rows = N // GRID_W
            ug2 = ug.reshape(B, rows, GRID_W, G)
            s_r, cnt_r = _window_sum(ug2, w, 1)
            s_rc, cnt_c = _window_sum(s_r, w, 2)
            mean = (s_rc / (cnt_r[:, None] * cnt_c[None, :])[None, :, :, None]).reshape(B, N, G)
        else:
            s, cnt = _window_sum(ug, w, 1)
            mean = s / cnt[None, :, None]
        outs.append(jnp.einsum('bnc,cd->bnd', (mean - ug).astype(u.dtype), w_pool[g]))
    return jnp.concatenate(outs, axis=-1) * scale


def _short_conv(x, w, b):
    n = x.shape[1]
    pad = CONV_W // 2
    xp = jnp.pad(x, ((0, 0), (pad, pad), (0, 0)))
    y = b
    for j in range(CONV_W):
        y = y + xp[:, j:j + n] * w[j]
    return y


def _to_heads(a):
    B, N, _ = a.shape
    return a.reshape(B, N, M_HEADS, M_HEAD_DIM).transpose(0, 2, 1, 3).astype(jnp.float32)


def _flip_t(a):
    return jnp.flip(a, axis=2)


def _mlstm_prepare(q, k, v, gates_pre, conv_w, conv_b, b_gates):
    qk = jax.nn.silu(_short_conv(jnp.concatenate([q, k], axis=-1), conv_w, conv_b))
    q, k = jnp.split(qk, 2, axis=-1)
    g = (gates_pre + b_gates).astype(jnp.float32).transpose(0, 2, 1)
    i_f, f_f, i_b, f_b = jnp.split(g, 4, axis=1)
    return _to_heads(q) * M_HEAD_DIM ** -0.5, _to_heads(k), _to_heads(v), (i_f, f_f, i_b, f_b)


def _mlstm_scan(q, k, v, i_pre, f_pre, state):
    B, H, N, dk = q.shape
    dv = v.shape[-1]
    L = CHUNK
    nc = N // L
    logf = jax.nn.log_sigmoid(f_pre)

    def to_chunks(a):
        return jnp.moveaxis(a.reshape(B, H, nc, L, *a.shape[3:]), 2, 0)

    mask = jnp.tril(jnp.ones((L, L), dtype=bool))

    def step(carry, inp):
        C0, n0, m0 = carry
        qc, kc, vc, ic, fc = inp
        b = jnp.cumsum(fc, axis=-1)
        a = b + m0[..., None]
        Dm = jnp.where(mask, b[..., :, None] - b[..., None, :] + ic[..., None, :], -jnp.inf)
        m = jnp.maximum(a, jnp.max(Dm, axis=-1))
        S = jnp.einsum('bhtd,bhsd->bhts', qc, kc) * jnp.exp(Dm - m[..., None])
        sa = jnp.exp(a - m)
        num = jnp.einsum('bhts,bhsv->bhtv', S, vc) + sa[..., None] * jnp.einsum('bhtd,bhdv->bhtv', qc, C0)
        den = jnp.sum(S, axis=-1) + sa * jnp.einsum('bhtd,bhd->bht', qc, n0)
        h = num / jnp.maximum(jnp.abs(den), jnp.exp(-m))[..., None]
        bL = b[..., -1]
        w = bL[..., None] - b + ic
        m_new = jnp.maximum(bL + m0, jnp.max(w, axis=-1))
        ws = jnp.exp(w - m_new[..., None])
        s0 = jnp.exp(bL + m0 - m_new)
        C_new = s0[..., None, None] * C0 + jnp.einsum('bhsd,bhsv->bhdv', kc * ws[..., None], vc)
        n_new = s0[..., None] * n0 + jnp.einsum('bhs,bhsd->bhd', ws, kc)
        return (C_new, n_new, m_new), h

    state, hs = lax.scan(step, state, (to_chunks(q), to_chunks(k), to_chunks(v), to_chunks(i_pre), to_chunks(logf)))
    h = jnp.moveaxis(hs, 0, 2).reshape(B, H, N, dv)
    return h, state


def _mlstm_bidir(q, k, v, gates, state_f, state_b):
    i_f, f_f, i_b, f_b = gates
    h_f, st_f = _mlstm_scan(q, k, v, i_f, f_f, state_f)
    h_b, st_b = _mlstm_scan(_flip_t(q), _flip_t(k), _flip_t(v), _flip_t(i_b), _flip_t(f_b), state_b)
    return h_f + _flip_t(h_b), st_f, st_b


def _mlstm_readout(h, o, gain):
    B, H, N, dh = h.shape
    h = h.transpose(0, 2, 1, 3)
    h = h * lax.rsqrt(jnp.mean(h * h, axis=-1, keepdims=True) + EPS)
    h = h.reshape(B, N, H * dh).astype(o.dtype) * gain
    return h * jax.nn.sigmoid(o)


def _token_mixer(a_lat, a_ctx, w_in, b_gates, conv_w, conv_b, w_pool, pool_scale, mh_gain, w_out, with_ctx_out):
    split_at = np.cumsum([POOL_WIDTH, M_WIDTH, M_WIDTH, M_WIDTH, M_WIDTH, D_MODEL, D_MODEL]).tolist()
    p_lat = jnp.split(a_lat @ w_in, split_at, axis=-1)
    p_ctx = jnp.split(a_ctx @ w_in, split_at, axis=-1)
    qc, kc, vc, gc = _mlstm_prepare(p_ctx[1], p_ctx[2], p_ctx[3], p_ctx[7], conv_w, conv_b, b_gates)
    ql, kl, vl, gl = _mlstm_prepare(p_lat[1], p_lat[2], p_lat[3], p_lat[7], conv_w, conv_b, b_gates)
    B = a_ctx.shape[0]
    state0 = (jnp.zeros((B, M_HEADS, M_HEAD_DIM, M_HEAD_DIM), jnp.float32),
              jnp.zeros((B, M_HEADS, M_HEAD_DIM), jnp.float32),
              jnp.zeros((B, M_HEADS), jnp.float32))
    h_ctx, st_f, st_b = _mlstm_bidir(qc, kc, vc, gc, state0, state0)
    h_lat, _, _ = _mlstm_bidir(ql, kl, vl, gl, st_f, st_b)

    def merge(p, h, on_grid):
        pool_out = _pool_mixer(p[0], w_pool, pool_scale, on_grid)
        m_out = _mlstm_readout(h, p[4], mh_gain)
        y = jax.nn.sigmoid(p[5]) * pool_out + jax.nn.sigmoid(p[6]) * m_out
        return y @ w_out

    mix_lat = merge(p_lat, h_lat, True)
    mix_ctx = merge(p_ctx, h_ctx, False) if with_ctx_out else None
    return mix_lat, mix_ctx


def setup_inputs(seed: int = 0) -> dict:
    key = jax.random.key(seed)
    ks = jax.random.split(key, 24)

    def nrm(k, shape, scale):
        return jax.random.normal(k, shape, jnp.float32) * scale

    G = POOL_GROUP_DIM
    gi = nrm(ks[11], (DEPTH, 2, M_HEADS), 0.1)
    gf = jnp.linspace(3.0, 6.0, M_HEADS, dtype=jnp.float32) + nrm(ks[12], (DEPTH, 2, M_HEADS), 0.1)
    b_gates = jnp.stack([gi, gf], axis=2).reshape(DEPTH, N_GATE_COLS)
    return {
        "x": nrm(ks[0], (BATCH, SEQ, D_MODEL), 1.0),
        "c": nrm(ks[1], (BATCH, D_MODEL), 1.0),
        "ctx": nrm(ks[2], (BATCH, CTX_LEN, D_MODEL), 1.0),
        "c_ctx": nrm(ks[3], (D_MODEL,), 1.0),
        "norm_mix": 1.0 + nrm(ks[4], (DEPTH, D_MODEL), 0.02),
        "norm_ffn": 1.0 + nrm(ks[5], (DEPTH, D_MODEL), 0.02),
        "norm_final": 1.0 + nrm(ks[6], (D_MODEL,), 0.02),
        "w_ada": nrm(ks[7], (DEPTH, D_MODEL, 6 * D_MODEL), 0.5 * D_MODEL ** -0.5),
        "b_ada": nrm(ks[8], (DEPTH, 6 * D_MODEL), 0.01),
        "w_in": nrm(ks[9], (DEPTH, D_MODEL, IN_COLS), D_MODEL ** -0.5),
        "b_gates": b_gates,
        "conv_w": nrm(ks[13], (DEPTH, CONV_W, 2 * M_WIDTH), CONV_W ** -0.5),
        "conv_b": nrm(ks[14], (DEPTH, 2 * M_WIDTH), 0.01),
        "w_pool": nrm(ks[15], (DEPTH, POOL_GROUPS, G, G), G ** -0.5),
        "pool_scale": 1.0 + nrm(ks[16], (DEPTH, POOL_WIDTH), 0.1),
        "mh_gain": 1.0 + nrm(ks[17], (DEPTH, M_WIDTH), 0.02),
        "w_out": nrm(ks[18], (DEPTH, D_MODEL, D_MODEL), D_MODEL ** -0.5),
        "w_ffn_in": nrm(ks[19], (DEPTH, D_MODEL, 2 * D_FF), D_MODEL ** -0.5),
        "w_ffn_out": nrm(ks[20], (DEPTH, D_FF, D_MODEL), D_FF ** -0.5),
    }


def reference(x, c, ctx, c_ctx, norm_mix, norm_ffn, norm_final, w_ada, b_ada, w_in, b_gates, conv_w, conv_b,
              w_pool, pool_scale, mh_gain, w_out, w_ffn_in, w_ffn_out):
    xc = ctx
    for l in range(DEPTH):
        last = l == DEPTH - 1
        mod = jax.nn.silu(c) @ w_ada[l] + b_ada[l]
        mod_c = jax.nn.silu(c_ctx) @ w_ada[l] + b_ada[l]
        sh1, sc1, g1, sh2, sc2, g2 = jnp.split(mod[:, None, :], 6, axis=-1)
        csh1, csc1, cg1, csh2, csc2, cg2 = jnp.split(mod_c, 6)
        a_lat = _modulate(x, norm_mix[l], sh1, sc1)
        a_ctx = _modulate(xc, norm_mix[l], csh1, csc1)
        mix_lat, mix_ctx = _token_mixer(a_lat, a_ctx, w_in[l], b_gates[l], conv_w[l], conv_b[l], w_pool[l],
                                        pool_scale[l], mh_gain[l], w_out[l], not last)
        x = x + g1 * mix_lat
        x = x + g2 * _swiglu(_modulate(x, norm_ffn[l], sh2, sc2), w_ffn_in[l], w_ffn_out[l])
        if not last:
            xc = xc + cg1 * mix_ctx
            xc = xc + cg2 * _swiglu(_modulate(xc, norm_ffn[l], csh2, csc2), w_ffn_in[l], w_ffn_out[l])
    return _rmsnorm(x, norm_final)
```

```python
import os
import numpy as np
import ml_dtypes
from contextlib import ExitStack

import concourse.bass as bass
import concourse.mybir as mybir
from concourse.bass_utils import run_bass_kernel_spmd

F32 = mybir.dt.float32
BF16 = mybir.dt.bfloat16
ALU = mybir.AluOpType
AF = mybir.ActivationFunctionType

D = 1024
SEQ = 4096
CTX = 256
NT = SEQ + CTX
NCH = NT // 128
DFF = 2816
NFF = DFF // 128
EPS = 1e-6
WINS = (2, 4, 8, 16)
HALO = (1, 1, 2, 4)
PADW = 1 + SEQ + 2 + CTX + 1
ENGS = ("pe", "act", "dve", "pool", "sp")


def skewed(n, stages):
    m = max(sk for _, sk in stages)
    for step in range(n + m):
        for fn, sk in stages:
            T = step - sk
            if 0 <= T < n:
                fn(T)


def pos_of_chunk(c):
    return 1 + 128 * c if c < 32 else (SEQ + 3) + 128 * (c - 32)


class Buf:
    __slots__ = ("w", "r", "excl")

    def __init__(self, excl=False):
        self.w = None
        self.r = {}
        self.excl = excl


def bufs(n):
    return [Buf() for _ in range(n)]


class DSem:
    __slots__ = ("id", "count")

    def __init__(self, i):
        self.id = i
        self.count = 0


class _Rec:
    def __getattr__(self, name):
        return lambda *a, **k: (name, a, k)


_REC = _Rec()


class Plan:
    def __init__(self):
        self.ops = {e: [] for e in ENGS}
        self.waited = {e: {} for e in ENGS}
        self.dsems = []
        self.disabled = False

    def dsem(self):
        d = DSem(len(self.dsems))
        self.dsems.append(d)
        return d

    def _deps(self, eng, reads, writes, skip_key=None):
        deps = {}
        ex = [b for b in reads if b.excl]
        if ex:
            reads = [b for b in reads if not b.excl]
            writes = list(writes) + ex

        def add(d, war=False):
            key, val = d
            if key == ("e", eng) and (eng in ("pe", "sp") or war):
                return
            if key == skip_key:
                return
            if deps.get(key, 0) < val:
                deps[key] = val

        for b in reads:
            if b.w is not None:
                add(b.w)
        for b in writes:
            if b.w is not None:
                add(b.w)
            for k, v in b.r.items():
                add((k, v), war=True)
        waits = []
        wd = self.waited[eng]
        for key, val in deps.items():
            if wd.get(key, 0) >= val:
                continue
            wd[key] = val
            waits.append((key, val))
        return waits

    def _mark(self, me, reads, writes):
        key, val = me
        ex = [b for b in reads if b.excl]
        if ex:
            reads = [b for b in reads if not b.excl]
            writes = list(writes) + ex
        for b in reads:
            if b.r.get(key, 0) < val:
                b.r[key] = val
        for b in writes:
            b.w = me
            b.r = {}

    def op(self, eng, fn, reads=(), writes=()):
        if self.disabled:
            return None
        waits = self._deps(eng, reads, writes)
        idx = len(self.ops[eng]) + 1
        self.ops[eng].append((fn(_REC), waits, None))
        me = (("e", eng), idx)
        self._mark(me, reads, writes)
        return me

    def dma(self, q, fn, ds, reads=(), writes=()):
        if self.disabled:
            return None
        waits = self._deps(q, reads, writes, skip_key=("d", ds.id))
        ds.count += 1
        self.ops[q].append((fn(_REC), waits, ds))
        me = (("d", ds.id), 16 * ds.count)
        self._mark(me, reads, writes)
        return me

    def barrier(self):
        if self.disabled:
            return
        deps = []
        for e in ENGS:
            if e != "sp" and len(self.ops[e]) > 0:
                idx = len(self.ops[e])
                while idx > 0 and (self.ops[e][idx - 1][0] is None or self.ops[e][idx - 1][2] is not None):
                    idx -= 1
                if idx > 0:
                    deps.append((("e", e), idx))
        for d in self.dsems:
            if d.count:
                deps.append((("d", d.id), 16 * d.count))
        for e in ENGS:
            waits = []
            for key, val in deps:
                if key == ("e", e):
                    continue
                if self.waited[e].get(key, 0) < val:
                    self.waited[e][key] = val
                    waits.append((key, val))
            if waits:
                self.ops[e].append((None, waits, None))

    def wait_all(self, eng, deps):
        if self.disabled:
            return
        waits = []
        for key, val in deps:
            if self.waited[eng].get(key, 0) < val:
                self.waited[eng][key] = val
                waits.append((key, val))
        self.ops[eng].append((None, waits, None))

    def emit(self, nc, stack):
        ms = {e: set() for e in ENGS}
        for e in ENGS:
            for fn, waits, ds in self.ops[e]:
                for key, val in waits:
                    if key[0] == "e":
                        ms[key[1]].add(val)
        rank = {e: {idx: i + 1 for i, idx in enumerate(sorted(ms[e]))} for e in ENGS}
        esem = {e: stack.enter_context(nc.semaphore(f"es_{e}")) for e in ENGS}
        dsem = [stack.enter_context(nc.semaphore(f"ds_{d.id}")) for d in self.dsems]
        self.stats = {e: (len(self.ops[e]), len(ms[e])) for e in ENGS}

        def replay(h, e):
            myrank = rank[e]
            for i, (fn, waits, ds) in enumerate(self.ops[e]):
                for key, val in waits:
                    if key[0] == "e":
                        h.wait_ge(esem[key[1]], rank[key[1]][val])
                    else:
                        h.wait_ge(dsem[key[1]], val)
                if fn is None:
                    continue
                ins = getattr(h, fn[0])(*fn[1], **fn[2])
                if ds is not None:
                    ins.then_inc(dsem[ds.id], 16)
                elif (i + 1) in myrank:
                    ins.then_inc(esem[e], 1)

        with nc.Block() as block:
            @block.tensor
            def _(h):
                replay(h, "pe")

            @block.scalar
            def _(h):
                replay(h, "act")

            @block.vector
            def _(h):
                replay(h, "dve")

            @block.gpsimd
            def _(h):
                replay(h, "pool")

            @block.sync
            def _(h):
                replay(h, "sp")


def pool_constants():
    blocks = []
    index = {}
    l = np.arange(128)
    rl, cl = l // 64, l % 64
    for wi, w in enumerate(WINS):
        hw = w // 2
        for o in range(-HALO[wi], HALO[wi] + 1):
            dr = 2 * o + rl[:, None] - rl[None, :]
            dc = cl[:, None] - cl[None, :]
            m = ((dr >= -hw) & (dr <= hw - 1) & (dc >= -hw) & (dc <= hw - 1)).astype(np.float32)
            if m.any():
                index[(wi, o)] = len(blocks)
                blocks.append(m)
    poolB = np.stack(blocks, axis=1)
    inv = np.zeros((128, 4, 32), np.float32)
    for wi, w in enumerate(WINS):
        hw = w // 2
        for T in range(32):
            r = 2 * T + rl
            cr = np.minimum(r + hw, 64) - np.maximum(r - hw, 0)
            cc = np.minimum(cl + hw, 64) - np.maximum(cl - hw, 0)
            inv[:, wi, T] = 1.0 / (cr * cc)
    return poolB.astype(ml_dtypes.bfloat16), index, inv


class _Stop(Exception):
    pass


def build_program(dbg=False, stop=None):
    nc = bass.Bass("TRN2", target_bir_lowering=False)
    P = Plan()

    def din(name, shape, dt=F32):
        return nc.dram_tensor(name, list(shape), dt, kind="ExternalInput").ap()

    x = din("x", [SEQ, D])
    ctx = din("ctx", [CTX, D])
    c_in = din("c", [1, D])
    cctx_in = din("c_ctx", [1, D])
    norm_mix = din("norm_mix", [1, D])
    norm_ffn = din("norm_ffn", [1, D])
    norm_final = din("norm_final", [1, D])
    w_ada = din("w_ada", [D, 6 * D])
    b_ada = din("b_ada", [1, 6 * D])
    w_in = din("w_in", [D, 7184])
    b_gates = din("b_gates", [1, 16])
    conv_w = din("conv_w", [3, 2048])
    conv_b = din("conv_b", [1, 2048])
    w_pool = din("w_pool", [4, 256, 256])
    pool_scale = din("pool_scale", [1, D])
    mh_gain = din("mh_gain", [1, D])
    w_out = din("w_out", [D, D])
    w_ffn_in = din("w_ffn_in", [D, 2 * DFF])
    w_ffn_out = din("w_ffn_out", [DFF, D])
    poolB_np, pidx, _ = pool_constants()
    NBLK = poolB_np.shape[1]
    ident_in = din("ident", [128, 128], BF16)
    maskf_in = din("maskf", [128, 128])
    maskb_in = din("maskb", [128, 128])
    trif_in = din("trif", [128, 128], BF16)
    trib_in = din("trib", [128, 128], BF16)
    ones_in = din("onesm", [128, 128], BF16)
    poolB_in = din("poolB", [128, NBLK, 128], BF16)
    invc_in = din("invc", [128, 4, 32])

    y_out = nc.dram_tensor("y", [SEQ, D], F32, kind="ExternalOutput").ap()
    skind = "ExternalOutput" if dbg else "Internal"
    aT_scr = nc.dram_tensor("aT_scr", [9, 128, 8, 512], BF16, kind=skind).ap()
    yT_scr = nc.dram_tensor("yT_scr", [8, 128, 8, 512], BF16, kind=skind).ap()
    rows_scr = nc.dram_tensor("rows_scr", [2, 6, D], F32, kind=skind).ap()
    wfi_scr = nc.dram_tensor("wfi_scr", [NFF, 128, 8 * 256], BF16, kind="Internal").ap()
    wo_scr = nc.dram_tensor("wo_scr", [8, 128, D], BF16, kind="Internal").ap()
    wfo_scr = nc.dram_tensor("wfo_scr", [NFF, 128, D], BF16, kind="Internal").ap()

    def stage(name):
        if stop == name:
            P.barrier()
            P.disabled = True

    rr = [0]
    uid = [0]

    def un(name):
        uid[0] += 1
        return f"s{uid[0]}_{name}"

    def evac_eng():
        rr[0] ^= 1
        return "act" if rr[0] else "dve"

    def copy_op(eng, out, in_, reads, writes):
        if eng == "act":
            return P.op("act", lambda h: h.activation(out=out, in_=in_, func=AF.Copy), reads, writes)
        return P.op(eng, lambda h: h.tensor_copy(out=out, in_=in_), reads, writes)

    def mm(out, lhsT, rhs, start, stop, reads, writes):
        return P.op("pe", lambda h: h.matmul(out, lhsT=lhsT, rhs=rhs, start=start, stop=stop), reads, writes)

    def tr(out, in_, ident, reads, writes):
        return P.op("pe", lambda h: h.transpose(out=out, in_=in_, identity=ident), reads, writes)

    dq = [0]

    def dmaq():
        dq[0] ^= 1
        return "sp"

    with ExitStack() as gst:
        def gsb(name, shape, dt):
            return gst.enter_context(nc.sbuf_tensor(un(name), list(shape), dt))

        banks = [gst.enter_context(nc.psum_tensor(f"bank{i}", [128, 512], F32)) for i in range(6)]
        bank67 = gst.enter_context(nc.psum_tensor("bank67", [128, 1024], F32))
        banks.append(bank67[:, 0:512])
        banks.append(bank67[:, 512:1024])
        BK = [Buf(excl=True) for _ in range(8)]

        ident = gsb("ident", [128, 128], BF16)
        maskf = gsb("maskf", [128, 128], F32)
        maskb = gsb("maskb", [128, 128], F32)
        trif = gsb("trif", [128, 128], BF16)
        trib = gsb("trib", [128, 128], BF16)
        onesm = gsb("onesm", [128, 128], BF16)
        Bconst = Buf()
        _bsem = {}
        _named = {}

        def nsem(name):
            if name not in _named:
                _named[name] = P.dsem()
            return _named[name]

        def dsem_for(b):
            if id(b) not in _bsem:
                _bsem[id(b)] = (P.dsem(), b)
            return _bsem[id(b)][0]

        for dst, src in ((ident, ident_in), (maskf, maskf_in), (maskb, maskb_in), (trif, trif_in),
                         (trib, trib_in), (onesm, ones_in)):
            P.dma("sp", lambda h, dst=dst, src=src: h.dma_start(out=dst[:], in_=src), dsem_for(Bconst), writes=[Bconst])
        erow16 = gsb("erow16", [128, 2, NCH, 4], F32)
        ecol = gsb("ecol", [128, 2, NCH, 4], F32)
        wsv = gsb("wsv", [128, 2, NCH, 4], F32)
        ebl = gsb("ebl", [128, 2, NCH, 4], F32)
        Btab = Buf()
        epsc = gsb("epsc", [128, 1], F32)
        Beps = Buf()
        P.op("dve", lambda h: h.memset(epsc[:], EPS), writes=[Beps])
        Bwfi = Buf()
        Bwos = Buf()
        G2c = gsb("G2c", [128, 8], F32)
        sh2c = gsb("sh2c", [128, 8], F32)
        sh2cb = gsb("sh2cb", [128, 8], BF16)
        fbias = gsb("fbias", [128, 2 * NFF], F32)
        Bcol = Buf()
        Bfb = Buf()

        try:
            with ExitStack() as st:
                def sb(name, shape, dt):
                    return st.enter_context(nc.sbuf_tensor(un(name), list(shape), dt))

                c2f = sb("c2f", [128, 8, 2], F32)
                c2b = sb("c2b", [128, 8, 2], BF16)
                Bc2 = Buf()
                P.dma("sp", lambda h: h.dma_start(out=c2f[:, :, 0:1], in_=c_in.rearrange("o (k p) -> p k o", p=128),
                                                  allow_slow_non_contiguous=True), dsem_for(Bc2), writes=[Bc2])
                P.dma("sp", lambda h: h.dma_start(out=c2f[:, :, 1:2], in_=cctx_in.rearrange("o (k p) -> p k o", p=128),
                                                  allow_slow_non_contiguous=True), dsem_for(Bc2), writes=[Bc2])
                P.op("act", lambda h: h.activation(out=c2b[:], in_=c2f[:], func=AF.Silu), reads=[Bc2], writes=[Bc2])
                bada = sb("bada", [2, 6 * D], F32)
                nmix = sb("nmix", [2, D], F32)
                nffn = sb("nffn", [2, D], F32)
                Bb = Buf()
                P.dma("sp", lambda h: h.dma_start(out=bada[:], in_=b_ada.partition_broadcast(2)), dsem_for(Bb), writes=[Bb])
                P.dma("sp", lambda h: h.dma_start(out=nmix[:], in_=norm_mix.partition_broadcast(2)), dsem_for(Bb), writes=[Bb])
                P.dma("sp", lambda h: h.dma_start(out=nffn[:], in_=norm_ffn.partition_broadcast(2)), dsem_for(Bb), writes=[Bb])
                modrow = sb("modrow", [2, 6 * D], F32)
                Bmod = Buf()
                wst = [sb(f"wst{i}", [128, 8, 512], F32) for i in range(4)]
                wbf = [sb(f"wbf{i}", [128, 8, 512], BF16) for i in range(4)]
                Bwst, Bwbf = bufs(4), bufs(4)
                dwst = [P.dsem() for _ in range(4)]
                pm = banks[0]
                Bpm = BK[0]
                for blk in range(12):
                    s = blk % 4
                    P.dma(dmaq(), lambda h, s=s, blk=blk: h.dma_start(
                        out=wst[s][:], in_=w_ada[:, blk * 512:(blk + 1) * 512].rearrange("(k p) c -> p k c", p=128)),
                        dwst[s], writes=[Bwst[s]])
                    copy_op(evac_eng(), wbf[s][:], wst[s][:], [Bwst[s]], [Bwbf[s]])
                    for k in range(8):
                        mm(pm[0:2, :], c2b[:, k, :], wbf[s][:, k, :], k == 0, k == 7, [Bc2, Bwbf[s]], [Bpm])
                    P.op("dve", lambda h, blk=blk: h.tensor_tensor(
                        out=modrow[:, blk * 512:(blk + 1) * 512], in0=pm[0:2, :], in1=bada[:, blk * 512:(blk + 1) * 512],
                        op=ALU.add), reads=[Bpm, Bb], writes=[Bmod])
                rows = sb("rows", [2, 6, D], F32)
                Brows = Buf()
                P.op("dve", lambda h: h.scalar_tensor_tensor(out=rows[:, 0, :], in0=modrow[:, D:2 * D], scalar=1.0,
                                                            in1=nmix[:], op0=ALU.add, op1=ALU.mult),
                     reads=[Bmod, Bb], writes=[Brows])
                P.op("dve", lambda h: h.scalar_tensor_tensor(out=rows[:, 3, :], in0=modrow[:, 4 * D:5 * D], scalar=1.0,
                                                            in1=nffn[:], op0=ALU.add, op1=ALU.mult),
                     reads=[Bmod, Bb], writes=[Brows])
                for slot, src in ((1, 0), (2, 2), (4, 3), (5, 5)):
                    P.op("dve", lambda h, slot=slot, src=src: h.tensor_copy(out=rows[:, slot, :],
                                                                              in_=modrow[:, src * D:(src + 1) * D]),
                         reads=[Bmod], writes=[Brows])
                Brscr = Buf()
                drows = P.dsem()
                P.dma("sp", lambda h: h.dma_start(out=rows_scr, in_=rows[:]), drows, reads=[Brows], writes=[Brscr])

            P.dma("sp", lambda h: h.dma_start(out=G2c[:], in_=rows_scr[0, 3, :].rearrange("(k p) -> p k", p=128),
                                              allow_slow_non_contiguous=True), dsem_for(Bcol), reads=[Brscr], writes=[Bcol])
            P.dma("sp", lambda h: h.dma_start(out=sh2c[:], in_=rows_scr[0, 4, :].rearrange("(k p) -> p k", p=128),
                                              allow_slow_non_contiguous=True), dsem_for(Bcol), reads=[Brscr], writes=[Bcol])
            P.op("dve", lambda h: h.tensor_copy(out=sh2cb[:], in_=sh2c[:]), reads=[Bcol], writes=[Bcol])
            P.barrier()
            stage("M")

            p1st = ExitStack()
            wqk2 = [p1st.enter_context(nc.sbuf_tensor(un(f"wqk{i}"), [128, 8, 512], BF16)) for i in range(2)]
            wv2 = [p1st.enter_context(nc.sbuf_tensor(un(f"wv{i}"), [128, 8, 256 + 16], BF16)) for i in range(2)]
            wu = p1st.enter_context(nc.sbuf_tensor(un("wu"), [128, 8, 256], BF16))
            wogg = p1st.enter_context(nc.sbuf_tensor(un("wogg"), [128, 8, 768], BF16))
            Bwqk2, Bwv2 = bufs(2), bufs(2)
            Bwu, Bwogg = Buf(), Buf()
            with ExitStack() as st:
                def sb(name, shape, dt):
                    return st.enter_context(nc.sbuf_tensor(un(name), list(shape), dt))

                cols1 = sb("cols1", [128, 4, 8], F32)
                Brow0 = Buf()
                for ci, (r, slot) in enumerate(((0, 0), (0, 1), (1, 0), (1, 1))):
                    P.dma("sp", lambda h, ci=ci, r=r, slot=slot: h.dma_start(
                        out=cols1[:, ci, :], in_=rows_scr[r, slot, :].rearrange("(k p) -> p k", p=128),
                        allow_slow_non_contiguous=True), dsem_for(Brow0), reads=[Brscr], writes=[Brow0])
                xt = [sb(f"xt{i}", [128, D], F32) for i in range(8)]
                Bxt = bufs(8)
                dxt = [P.dsem() for _ in range(8)]
                junk = sb("junk", [128, D], BF16)
                Bjunk = Buf()
                ssa = sb("ssa", [128, NCH], F32)
                rsa = sb("rsa", [128, NCH], F32)
                Bss = bufs(NCH)
                P.op("pool", lambda h: h.memset(ssa[:], 0.0), writes=Bss)
                abf = [sb(f"abf{i}", [128, D], BF16) for i in range(3)]
                Babf = bufs(3)
                stg = [sb(f"stg{i}", [128, 8, 512], BF16) for i in range(2)]
                Bstg = bufs(2)
                dstg = [P.dsem() for _ in range(2)]
                pT = [banks[i][:].bitcast(BF16) for i in range(3)]
                BpT = [BK[0], BK[1], BK[2]]
                BaT = bufs(9)
                def p0_load(T):
                    s4 = T % 8
                    src = x[T * 128:(T + 1) * 128, :] if T < 32 else ctx[(T - 32) * 128:(T - 31) * 128, :]
                    P.dma(dmaq(), lambda h: h.dma_start(out=xt[s4][:], in_=src), dxt[s4], writes=[Bxt[s4]])

                def p0_stats(T):
                    s4 = T % 8
                    P.op("act", lambda h: h.activation(out=junk[:], in_=xt[s4][:], func=AF.Square, accum_out=ssa[:, T:T + 1]),
                         reads=[Bxt[s4]], writes=[Bjunk, Bss[T]])
                    P.op("act", lambda h: h.activation(out=rsa[:, T:T + 1], in_=ssa[:, T:T + 1], func=AF.Sqrt, scale=1.0 / D,
                                                       bias=epsc[:]), reads=[Bss[T], Beps], writes=[Bss[T]])

                def p0_norm(T):
                    s4, s3 = T % 8, T % 3
                    P.op("dve", lambda h: h.reciprocal(out=rsa[:, T:T + 1], in_=rsa[:, T:T + 1]), reads=[Bss[T]], writes=[Bss[T]])
                    P.op("dve", lambda h: h.tensor_scalar_mul(out=abf[s3][:], in0=xt[s4][:], scalar1=rsa[:, T:T + 1]),
                         reads=[Bxt[s4], Bss[T]], writes=[Babf[s3]])

                def p0_reg(blk, k, t0, t1):
                    bnk = 4 * (blk % 2) + k // 2
                    return banks[bnk][:].bitcast(BF16)[:, (k % 2) * 512 + t0:(k % 2) * 512 + t1], BK[bnk]

                def p0_tr(T):
                    s3 = T % 3
                    blk, ti = T // 4, T % 4
                    for k in range(8):
                        reg, Bk = p0_reg(blk, k, ti * 128, (ti + 1) * 128)
                        tr(reg, abf[s3][:, k * 128:(k + 1) * 128], ident[:], [Babf[s3], Bconst], [Bk])

                def p0_out(T):
                    blk, ti = T // 4, T % 4
                    nt = 512 if blk < 8 else 256
                    if ti != (nt // 128) - 1:
                        return
                    ss_ = blk % 2
                    gi = 0 if blk < 8 else 2
                    for k in range(8):
                        reg, Bk = p0_reg(blk, k, 0, nt)
                        dst = stg[ss_][:, k, 0:nt]
                        if k < 4:
                            P.op("act", lambda h, dst=dst, reg=reg, k=k: h.activation(
                                out=dst, in_=reg, func=AF.Identity, scale=cols1[:, gi, k:k + 1], bias=cols1[:, gi + 1, k:k + 1]),
                                reads=[Bk, Brow0], writes=[Bstg[ss_]])
                        else:
                            P.op("dve", lambda h, dst=dst, reg=reg, k=k: h.tensor_scalar(
                                out=dst, in0=reg, scalar1=cols1[:, gi, k:k + 1], scalar2=cols1[:, gi + 1, k:k + 1],
                                op0=ALU.mult, op1=ALU.add), reads=[Bk, Brow0], writes=[Bstg[ss_]])

                def p0_store(T):
                    blk, ti = T // 4, T % 4
                    nt = 512 if blk < 8 else 256
                    if ti != (nt // 128) - 1:
                        return
                    ss_ = blk % 2
                    P.dma("sp", lambda h: h.dma_start(out=aT_scr[blk, :, :, 0:nt],
                                                      in_=stg[ss_][:, :, 0:nt]), dstg[ss_], reads=[Bstg[ss_]], writes=[BaT[blk]])

                w0st = [sb(f"w0st{i}", [128, 8, 256], F32) for i in range(2)]
                Bw0st = bufs(2)
                dw0 = [P.dsem() for _ in range(2)]
                h0_loads = [(wqk2[0], 0, 1024, 256, Bwqk2[0]), (wqk2[0], 256, 2048, 256, Bwqk2[0]),
                            (wv2[0], 0, 3072, 256, Bwv2[0]), (wv2[0], 256, 7168, 16, Bwv2[0])]

                def p0_w(T):
                    if T in (1, 3, 5, 7):
                        n_ = (T - 1) // 2
                        dst_, dcol_, col0_, ncol_, Bd_ = h0_loads[n_]
                        P.dma("sp", lambda h: h.dma_start(
                            out=w0st[n_ % 2][:, :, 0:ncol_],
                            in_=w_in[:, col0_:col0_ + ncol_].rearrange("(k p) c -> p k c", p=128)),
                            dw0[n_ % 2], writes=[Bw0st[n_ % 2]])
                    if T in (4, 6, 8, 10):
                        n_ = (T - 4) // 2
                        dst_, dcol_, col0_, ncol_, Bd_ = h0_loads[n_]
                        copy_op("dve", dst_[:, :, dcol_:dcol_ + ncol_], w0st[n_ % 2][:, :, 0:ncol_], [Bw0st[n_ % 2]], [Bd_])

                skewed(NCH, [(p0_store, 9), (p0_w, 0), (p0_load, 0), (p0_stats, 4), (p0_norm, 5), (p0_tr, 6), (p0_out, 7)])
            P.barrier()
            stage("0")

            ByT = bufs(8)
            for hd in range(4):
                with ExitStack() as st:
                    def sb(name, shape, dt):
                        return st.enter_context(nc.sbuf_tensor(un(name), list(shape), dt))

                    aTb = [sb(f"aTb{i}", [128, 8, 512], BF16) for i in range(2)]
                    BaTb = bufs(2)
                    daT = [nsem(f"daT{i}") for i in range(2)]
                    wst = [aTb[i][:].bitcast(F32) for i in range(2)]
                    Bwst = BaTb
                    dwst = daT
                    wqk, wv = wqk2[hd % 2], wv2[hd % 2]
                    Bwqk, Bwv = Bwqk2[hd % 2], Bwv2[hd % 2]
                    wi = [0]

                    def load_cols(dst, dcol, col0, ncol, Bdst):
                        s = wi[0] % 2
                        wi[0] += 1
                        P.dma(dmaq(), lambda h: h.dma_start(
                            out=wst[s][:, :, 0:ncol], in_=w_in[:, col0:col0 + ncol].rearrange("(k p) c -> p k c", p=128)),
                            dwst[s], writes=[Bwst[s]])
                        copy_op(evac_eng(), dst[:, :, dcol:dcol + ncol], wst[s][:, :, 0:ncol], [Bwst[s]], [Bdst])

                    nwqk, nwv = wqk2[(hd + 1) % 2], wv2[(hd + 1) % 2]
                    nBwqk, nBwv = Bwqk2[(hd + 1) % 2], Bwv2[(hd + 1) % 2]
                    bg_loads = [(wu, 0, 256 * hd, 256, Bwu), (wogg, 0, 4096 + 256 * hd, 256, Bwogg),
                                (wogg, 256, 5120 + 256 * hd, 256, Bwogg), (wogg, 512, 6144 + 256 * hd, 256, Bwogg)]
                    if hd < 3:
                        bg_loads += [(nwqk, 0, 1024 + 256 * (hd + 1), 256, nBwqk), (nwqk, 256, 2048 + 256 * (hd + 1), 256, nBwqk),
                                     (nwv, 0, 3072 + 256 * (hd + 1), 256, nBwv), (nwv, 256, 7168, 16, nBwv)]
                    stage(f"w1_{hd}")
                    cw = sb("cw", [128, 3, 4], F32)
                    cb = sb("cb", [128, 4], F32)
                    Bcw = Buf()
                    for m in range(4):
                        ch0 = (0 if m < 2 else 1024) + 256 * hd + 128 * (m % 2)
                        P.dma("sp", lambda h, m=m, ch0=ch0: h.dma_start(
                            out=cw[:, :, m:m + 1], in_=conv_w[:, ch0:ch0 + 128].rearrange("j (p o) -> p j o", o=1),
                            allow_slow_non_contiguous=True), dsem_for(Bcw), writes=[Bcw])
                        P.dma("sp", lambda h, m=m, ch0=ch0: h.dma_start(
                            out=cb[:, m:m + 1], in_=conv_b[:, ch0:ch0 + 128].rearrange("o (p q) -> p (o q)", q=1),
                            allow_slow_non_contiguous=True), dsem_for(Bcw), writes=[Bcw])
                    stage(f"w2_{hd}")
                    psc_b = sb("psc_b", [128, 256], F32)
                    gain_b = sb("gain_b", [128, 256], F32)
                    bg_b = sb("bg_b", [128, 16], F32)
                    Bsmall = Buf()
                    P.dma("sp", lambda h: h.dma_start(out=psc_b[:], in_=pool_scale[:, 256 * hd:256 * hd + 256].partition_broadcast(128)),
                          dsem_for(Bsmall), writes=[Bsmall])
                    P.dma("sp", lambda h: h.dma_start(out=gain_b[:], in_=mh_gain[:, 256 * hd:256 * hd + 256].partition_broadcast(128)),
                          dsem_for(Bsmall), writes=[Bsmall])
                    P.dma("sp", lambda h: h.dma_start(out=bg_b[:], in_=b_gates.partition_broadcast(128)), dsem_for(Bsmall), writes=[Bsmall])
                    stage(f"w3_{hd}")
                    wpf = sb("wpf", [128, 2, 256], F32)
                    wpb = sb("wpb", [128, 2, 256], BF16)
                    Bwp = Buf()
                    P.dma("sp", lambda h: h.dma_start(out=wpf[:], in_=w_pool[hd].rearrange("(c p) o -> p c o", p=128)),
                          dsem_for(Bwp), writes=[Bwp])
                    for cc_ in range(2):
                        P.op("dve", lambda h, cc_=cc_: h.tensor_tensor(out=wpb[:, cc_, :], in0=wpf[:, cc_, :], in1=psc_b[:],
                                                                      op=ALU.mult), reads=[Bwp, Bsmall], writes=[Bwp])

                    stage(f"w_{hd}")
                    qkT = sb("qkT", [128, 4, PADW], BF16)
                    BqkT = bufs(4)
                    vaug = sb("vaug", [128, NCH, 257], BF16)
                    Bv = bufs(NCH)
                    Bvone = Buf()
                    P.op("pool", lambda h: h.memset(vaug[:, :, 256:257], 1.0), writes=[Bvone])
                    ai = [0]

                    def load_aT(blk, nt):
                        s = ai[0] % 2
                        ai[0] += 1
                        P.dma(dmaq(), lambda h: h.dma_start(
                            out=aTb[s][:, :, 0:nt], in_=aT_scr[blk, :, :, 0:nt]),
                            daT[s], reads=[BaT[blk]], writes=[BaTb[s]])
                        return s

                    with ExitStack() as st1:
                        pre = st1.enter_context(nc.sbuf_tensor(un("pre"), [128, 4, PADW], BF16))
                        Bpre = bufs(4)
                        gpre = st1.enter_context(nc.sbuf_tensor(un("gpre"), [128, NCH, 16], F32))
                        Bgpre = Buf()
                        for (a0, a1) in ((0, 1), (SEQ + 1, SEQ + 3), (PADW - 1, PADW)):
                            P.op("pool", lambda h, a0=a0, a1=a1: h.memset(pre[:, :, a0:a1], 0.0), writes=Bpre)
                        Bpq = [BK[0], BK[1], BK[2]]
                        Bpv = [BK[3], BK[4]]
                        pqi = [0]
                        dgw = st1.enter_context(nc.sbuf_tensor(un("dgw"), [128, 3, 4, 128], BF16))
                        Bdgw = Buf()
                        for j_ in range(3):
                            for cc_ in range(4):
                                P.op("dve", lambda h, j_=j_, cc_=cc_: h.tensor_scalar_mul(
                                    out=dgw[:, j_, cc_, :], in0=ident[:], scalar1=cw[:, j_, cc_:cc_ + 1]),
                                    reads=[Bconst, Bcw], writes=[Bdgw])
                        cvi = [0]

                        def conv_block(cb_):
                            ntc = 512 if cb_ < 8 else 256
                            pc0 = pos_of_chunk(cb_ * 4)
                            for cc_ in range(4):
                                bnk = 5 + cvi[0] % 3
                                cvi[0] += 1
                                pcv = banks[bnk]
                                for j_ in range(3):
                                    mm(pcv[:, 0:ntc], dgw[:, j_, cc_, :], pre[:, cc_, pc0 + j_ - 1:pc0 + j_ - 1 + ntc],
                                       j_ == 0, j_ == 2, [Bdgw, Bpre[cc_]], [BK[bnk]])
                                P.op("act", lambda h, cc_=cc_, pcv=pcv: h.activation(
                                    out=qkT[:, cc_, pc0:pc0 + ntc], in_=pcv[:, 0:ntc], func=AF.Silu, bias=cb[:, cc_:cc_ + 1]),
                                    reads=[BK[bnk], Bcw], writes=[BqkT[cc_]])

                        NSL = 4
                        wst2 = [st1.enter_context(nc.sbuf_tensor(un(f"wst2_{r}"), [128, 8, 256], F32)) for r in range(2)]
                        wbs = [st1.enter_context(nc.sbuf_tensor(un(f"wbs_{r}"), [128, 8, 256], BF16)) for r in range(NSL)]
                        Bwst2, Bwbs = bufs(2), bufs(NSL)
                        dw2 = [nsem(f"dw2_{i}") for i in range(2)]
                        dw2o = [nsem(f"dw2o_{i}") for i in range(NSL)]
                        gg = [st1.enter_context(nc.sbuf_tensor(un("g1bb"), [128, D], F32)),
                              st1.enter_context(nc.sbuf_tensor(un("g2bb"), [128, D], F32))]
                        Bgg = Buf()
                        for dst, slot in ((gg[0], 2), (gg[1], 5)):
                            P.dma("sp", lambda h, dst=dst, slot=slot: h.dma_start(
                                out=dst[:], in_=rows_scr[0, slot:slot + 1, :].partition_broadcast(128)),
                                dsem_for(Bgg), reads=[Brscr], writes=[Bgg])
                        NPIECE = NFF + 4 + NFF // 2
                        per_head = -(-NPIECE // 4)
                        pieces = list(range(per_head * hd, min(per_head * (hd + 1), NPIECE)))

                        def v2(t):
                            return t[:].rearrange("p k c -> p (k c)").rearrange("p (a b) -> p a b", a=2)

                        def pw_dma(j):
                            s_ = j % 2
                            if j < NFF:
                                P.dma("sp", lambda h: h.dma_start(
                                    out=wst2[s_][:, :, 0:128], in_=w_ffn_in[:, j * 128:(j + 1) * 128].rearrange("(k p) c -> p k c", p=128)),
                                    dw2[s_], writes=[Bwst2[s_]])
                                P.dma("sp", lambda h: h.dma_start(
                                    out=wst2[s_][:, :, 128:256],
                                    in_=w_ffn_in[:, DFF + j * 128:DFF + (j + 1) * 128].rearrange("(k p) c -> p k c", p=128)),
                                    dw2[s_], writes=[Bwst2[s_]])
                            else:
                                src, k2 = (w_out, j - NFF) if j < NFF + 4 else (w_ffn_out, j - NFF - 4)
                                P.dma("sp", lambda h: h.dma_start(
                                    out=v2(wst2[s_]), in_=src[k2 * 256:(k2 + 1) * 256, :].rearrange("(k p) c -> p k c", p=128)),
                                    dw2[s_], writes=[Bwst2[s_]])

                        def pw_cast(j):
                            s_ = j % NSL
                            w_ = j % 2
                            if j < NFF:
                                copy_op("act", wbs[s_][:], wst2[w_][:], [Bwst2[w_]], [Bwbs[s_]])
                            else:
                                gb = gg[0] if j < NFF + 4 else gg[1]
                                P.op("dve", lambda h: h.tensor_tensor(out=v2(wbs[s_]), in0=v2(wst2[w_]),
                                                                      in1=gb[:].unsqueeze(1).to_broadcast([128, 2, D]), op=ALU.mult),
                                     reads=[Bwst2[w_], Bgg], writes=[Bwbs[s_]])

                        def pw_fin(j):
                            s_ = j % NSL
                            if j < NFF:
                                for half in range(2):
                                    col = 300 + 2 * j + half
                                    for k in range(8):
                                        mm(banks[4][:, col:col + 1], wbs[s_][:, k, half * 128:(half + 1) * 128], sh2cb[:, k:k + 1],
                                           k == 0, k == 7, [Bwbs[s_], Bcol], [BK[4]])
                                P.op("dve", lambda h: h.tensor_copy(out=fbias[:, 2 * j:2 * j + 2], in_=banks[4][:, 300 + 2 * j:302 + 2 * j]),
                                     reads=[BK[4]], writes=[Bfb])
                                P.dma("sp", lambda h: h.dma_start(out=wfi_scr[j].rearrange("p (k c) -> p k c", k=8), in_=wbs[s_][:]),
                                      dw2o[s_], reads=[Bwbs[s_]], writes=[Bwfi])
                            elif j < NFF + 4:
                                k2 = j - NFF
                                P.dma("sp", lambda h: h.dma_start(out=wo_scr[2 * k2:2 * k2 + 2].rearrange("k p c -> p k c"), in_=v2(wbs[s_])),
                                      dw2o[s_], reads=[Bwbs[s_]], writes=[Bwos])
                            else:
                                k2 = j - NFF - 4
                                P.dma("sp", lambda h: h.dma_start(out=wfo_scr[2 * k2:2 * k2 + 2].rearrange("k p c -> p k c"), in_=v2(wbs[s_])),
                                      dw2o[s_], reads=[Bwbs[s_]], writes=[Bwos])

                        def pw_stage(blk_):
                            for fn_, lag in ((pw_fin, 2), (pw_cast, 1), (pw_dma, 0)):
                                b_ = blk_ - lag
                                if 0 <= b_ < 5:
                                    for j_ in pieces[2 * b_:2 * b_ + 2]:
                                        fn_(j_)

                        for blk in range(9):
                            nt = 512 if blk < 8 else 256
                            s = load_aT(blk, nt)
                            pw_stage(blk)
                            if blk >= 2:
                                conv_block(blk - 2)
                            p0 = pos_of_chunk(blk * 4)
                            for cc in range(0 if not os.environ.get("SKIP_QK") else 4, 4):
                                b = pqi[0] % 3
                                pqi[0] += 1
                                pq = banks[b]
                                for k in range(8):
                                    mm(pq[:, 0:nt], wqk[:, k, cc * 128:(cc + 1) * 128], aTb[s][:, k, 0:nt], k == 0, k == 7,
                                       [Bwqk, BaTb[s]], [Bpq[b]])
                                copy_op(evac_eng(), pre[:, cc, p0:p0 + nt], pq[:, 0:nt], [Bpq[b]], [Bpre[cc]])
                            for ti in range(nt // 128 if not os.environ.get("SKIP_V") else 0):
                                ch = blk * 4 + ti
                                b = ch % 2
                                pv = banks[3 + b]
                                nco = 272 if (hd == 0 and not os.environ.get('SKIP_G')) else 256
                                for k in range(8):
                                    mm(pv[:, 0:256], aTb[s][:, k, ti * 128:(ti + 1) * 128], wv[:, k, 0:256], k == 0, k == 7,
                                       [Bwv, BaTb[s]], [Bpv[b]])
                                if nco > 256:
                                    for k in range(8):
                                        mm(pv[:, 256:272], aTb[s][:, k, ti * 128:(ti + 1) * 128], wv[:, k, 256:272], k == 0, k == 7,
                                           [Bwv, BaTb[s]], [Bpv[b]])
                                copy_op("act", vaug[:, ch, 0:256], pv[:, 0:256], [Bpv[b]], [Bv[ch]])
                                if hd == 0 and not os.environ.get('SKIP_G') and not os.environ.get('SKIP_GD'):
                                    P.op("dve", lambda h, ch=ch, pv=pv: h.tensor_tensor(
                                        out=gpre[:, ch, :], in0=pv[:, 256:272], in1=bg_b[:], op=ALU.add),
                                        reads=[Bpv[b], Bsmall], writes=[Bgpre])

                        conv_block(7)
                        conv_block(8)
                        stage(f"proj_{hd}")
                        if hd == 0:
                            lf = st1.enter_context(nc.sbuf_tensor(un("lf"), [128, 2, NCH, 4], F32))
                            lr = st1.enter_context(nc.sbuf_tensor(un("lr"), [128, 2, NCH, 4], F32))
                            lhi = st1.enter_context(nc.sbuf_tensor(un("lhi"), [128, 2, NCH, 4], BF16))
                            lmid = st1.enter_context(nc.sbuf_tensor(un("lmid"), [128, 2, NCH, 4], BF16))
                            llo = st1.enter_context(nc.sbuf_tensor(un("llo"), [128, 2, NCH, 4], BF16))
                            gt1 = st1.enter_context(nc.sbuf_tensor(un("gt1"), [128, 2, NCH, 4], F32))
                            gt2 = st1.enter_context(nc.sbuf_tensor(un("gt2"), [128, 2, NCH, 4], F32))
                            Blf, Bl3, Bg1, Bg2 = bufs(4)
                            for d in range(2):
                                P.op("act", lambda h, d=d: h.activation(out=lf[:, d], in_=gpre[:, :, 4 + 8 * d:8 + 8 * d],
                                                                      func=AF.Exp, scale=-1.0), reads=[Bgpre], writes=[Blf])
                            P.op("dve", lambda h: h.tensor_scalar_add(out=lf[:], in0=lf[:], scalar1=1.0), reads=[Blf], writes=[Blf])
                            P.op("act", lambda h: h.activation(out=lf[:], in_=lf[:], func=AF.Ln), reads=[Blf], writes=[Blf])
                            P.op("dve", lambda h: h.tensor_copy(out=lhi[:], in_=lf[:]), reads=[Blf], writes=[Bl3])
                            P.op("dve", lambda h: h.tensor_tensor(out=lr[:], in0=lf[:], in1=lhi[:], op=ALU.subtract),
                                 reads=[Blf, Bl3], writes=[Bg1])
                            P.op("dve", lambda h: h.tensor_copy(out=lmid[:], in_=lr[:]), reads=[Bg1], writes=[Bl3])
                            P.op("dve", lambda h: h.tensor_tensor(out=lr[:], in0=lr[:], in1=lmid[:], op=ALU.subtract),
                                 reads=[Bg1, Bl3], writes=[Bg1])
                            P.op("dve", lambda h: h.tensor_copy(out=llo[:], in_=lr[:]), reads=[Bg1], writes=[Bl3])
                            NG = NCH * 4
                            for d in range(2):
                                tri = trif if d == 0 else trib
                                pc = banks[5 + d][:, 0:NG]
                                ptot = banks[5 + d][:, NG:2 * NG]
                                BpC = BK[5 + d]
                                for i3, part in enumerate((lhi, lmid, llo)):
                                    mm(pc, tri[:], part[:, d].rearrange("p c g -> p (c g)"), i3 == 0, i3 == 2,
                                       [Bconst, Bl3], [BpC])
                                for i3, part in enumerate((lhi, lmid, llo)):
                                    mm(ptot, onesm[:], part[:, d].rearrange("p c g -> p (c g)"), i3 == 0, i3 == 2,
                                       [Bconst, Bl3], [BpC])
                            for d in range(2):
                                pc = banks[5 + d][:, 0:NG].rearrange("p (c g) -> p c g", g=4)
                                ptot = banks[5 + d][:, NG:2 * NG].rearrange("p (c g) -> p c g", g=4)
                                BpC = BK[5 + d]
                                P.op("act", lambda h, d=d, pc=pc: h.activation(out=erow16[:, d], in_=pc, func=AF.Exp, scale=-1.0),
                                     reads=[BpC], writes=[Btab])
                                P.op("dve", lambda h, d=d: h.tensor_scalar_mul(out=erow16[:, d], in0=erow16[:, d], scalar1=1.0 / 16),
                                     reads=[Btab], writes=[Btab])
                                P.op("dve", lambda h, d=d, pc=pc: h.tensor_tensor(out=gt1[:, d], in0=gpre[:, :, 8 * d:8 * d + 4],
                                                                                in1=pc, op=ALU.add),
                                     reads=[Bgpre, BpC], writes=[Bg1])
                                P.op("act", lambda h, d=d: h.activation(out=ecol[:, d], in_=gt1[:, d], func=AF.Exp),
                                     reads=[Bg1], writes=[Btab])
                                P.op("dve", lambda h, d=d, ptot=ptot: h.tensor_tensor(out=gt2[:, d], in0=gt1[:, d], in1=ptot,
                                                                                    op=ALU.subtract),
                                     reads=[Bg1, BpC], writes=[Bg2])
                                P.op("act", lambda h, d=d: h.activation(out=wsv[:, d], in_=gt2[:, d], func=AF.Exp),
                                     reads=[Bg2], writes=[Btab])
                                P.op("act", lambda h, d=d, ptot=ptot: h.activation(out=ebl[:, d], in_=ptot, func=AF.Exp, scale=-1.0),
                                     reads=[BpC], writes=[Btab])

                        stage(f"gates_{hd}")
                    P.barrier()
                    stage(f"s1_{hd}")

                    hacc = sb("hacc", [128, 32, 256], F32)
                    Bh = bufs(32)
                    with ExitStack() as st2:
                        def sb2(name, shape, dt):
                            return st2.enter_context(nc.sbuf_tensor(un(name), list(shape), dt))

                        RING = 4
                        Cb = [[sb2(f"Cb{d}_{r}", [128, 2, 257], BF16) for r in range(RING)] for d in range(2)]
                        BCb = [bufs(RING) for _ in range(2)]
                        kp = [[sb2(f"kp{d}_{r}", [128, 256], BF16) for r in range(2)] for d in range(2)]
                        Bkp = [bufs(2) for _ in range(2)]
                        dg = [[sb2(f"dg{d}_{r}", [128, 128], BF16) for r in range(2)] for d in range(2)]
                        Bdg = [bufs(2) for _ in range(2)]
                        PT = [[sb2(f"PT{d}_{r}", [128, 128], BF16) for r in range(2)] for d in range(2)]
                        BPT = [bufs(2) for _ in range(2)]
                        sc1 = sb2("sc1", [128, 2, NCH], F32)
                        sc2 = sb2("sc2", [128, 2, NCH], F32)
                        Bsc = [bufs(NCH) for _ in range(2)]
                        order = [[32, 33] + list(range(32)), [33, 32] + list(range(31, -1, -1))]
                        masks = [maskf, maskb]
                        hwritten = [False] * 32
                        for d in range(2):
                            P.op("pool", lambda h, d=d: h.memset(Cb[d][0][:], 0.0), writes=[BCb[d][0]])

                        def G(d, i):
                            if i >= NCH - 1:
                                return
                            c = order[d][i]
                            P.op("dve", lambda h: h.tensor_scalar_mul(out=dg[d][i % 2][:], in0=ident[:], scalar1=ebl[:, d, c, hd:hd + 1]),
                                 reads=[Bconst, Btab], writes=[Bdg[d][i % 2]])

                        def K0(d, i):
                            if i >= NCH - 1:
                                return
                            p0 = pos_of_chunk(order[d][i])
                            psT = banks[d][:, 0:128].bitcast(BF16)
                            for j in range(2):
                                tr(psT[:, j * 128:(j + 1) * 128], qkT[:, 2 + j, p0:p0 + 128], ident[:], [BqkT[2 + j], Bconst], [BK[d]])

                        def K1(d, i):
                            if i >= NCH - 1:
                                return
                            c = order[d][i]
                            psT = banks[d][:, 0:128].bitcast(BF16)
                            P.op("act", lambda h: h.activation(out=kp[d][i % 2][:], in_=psT[:, 0:256], func=AF.Copy,
                                                               scale=wsv[:, d, c, hd:hd + 1]), reads=[BK[d], Btab], writes=[Bkp[d][i % 2]])

                        def K2(d, i):
                            if i >= NCH - 1:
                                return
                            c = order[d][i]
                            for j in range(2):
                                psU = banks[6 + j]
                                BU = BK[6 + j]
                                mm(psU[:, 0:257], kp[d][i % 2][:, j * 128:(j + 1) * 128], vaug[:, c, :], True, False,
                                   [Bkp[d][i % 2], Bv[c], Bvone], [BU])
                                mm(psU[:, 0:257], dg[d][i % 2][:], Cb[d][i % RING][:, j, :], False, True, [Bdg[d][i % 2], BCb[d][i % RING]], [BU])

                        def K3(d, i):
                            if i >= NCH - 1:
                                return
                            rn = (i + 1) % RING
                            copy_op("act", Cb[d][rn][:], bank67[:].rearrange("p (j c) -> p j c", j=2)[:, :, 0:257],
                                    [BK[6], BK[7]], [BCb[d][rn]])

                        def O0(d, i):
                            c = order[d][i]
                            if c >= 32:
                                return
                            p0 = pos_of_chunk(c)
                            psS = banks[d][:, 128:256]
                            for j in range(2):
                                mm(psS, qkT[:, 2 + j, p0:p0 + 128], qkT[:, j, p0:p0 + 128], j == 0, j == 1, [BqkT[2 + j], BqkT[j]], [BK[d]])

                        def O1(d, i):
                            c = order[d][i]
                            if c >= 32:
                                return
                            psS = banks[d][:, 128:256]
                            P.op("dve", lambda h: h.scalar_tensor_tensor(
                                out=PT[d][i % 2][:], in0=psS, scalar=ecol[:, d, c, hd:hd + 1], in1=masks[d][:], op0=ALU.mult, op1=ALU.mult),
                                reads=[BK[d], Btab, Bconst], writes=[BPT[d][i % 2]])

                        def O2(d, i):
                            c = order[d][i]
                            if c >= 32:
                                return
                            p0 = pos_of_chunk(c)
                            psO = banks[2 + 2 * d + i % 2]
                            BO = BK[2 + 2 * d + i % 2]
                            mm(psO[:, 0:257], PT[d][i % 2][:], vaug[:, c, :], True, False, [BPT[d][i % 2], Bv[c], Bvone], [BO])
                            for j in range(2):
                                mm(psO[:, 0:257], qkT[:, j, p0:p0 + 128], Cb[d][i % RING][:, j, :], False, j == 1,
                                   [BqkT[j], BCb[d][i % RING]], [BO])

                        def O3(d, i):
                            c = order[d][i]
                            if c >= 32:
                                return
                            psO = banks[2 + 2 * d + i % 2]
                            BO = BK[2 + 2 * d + i % 2]
                            e16 = erow16[:, d, c, hd:hd + 1]
                            s1c = sc1[:, d, c:c + 1]
                            s2c = sc2[:, d, c:c + 1]
                            Bs = Bsc[d][c]
                            P.op("act", lambda h: h.activation(out=s1c, in_=psO[:, 256:257], func=AF.Abs, scale=e16),
                                 reads=[BO, Btab], writes=[Bs])
                            P.op("dve", lambda h: h.tensor_scalar_max(out=s1c, in0=s1c, scalar1=1.0), reads=[Bs], writes=[Bs])
                            P.op("dve", lambda h: h.reciprocal(out=s1c, in_=s1c), reads=[Bs], writes=[Bs])
                            P.op("dve", lambda h: h.tensor_tensor(out=s2c, in0=e16, in1=s1c, op=ALU.mult), reads=[Bs, Btab], writes=[Bs])
                            if not hwritten[c]:
                                hwritten[c] = True
                                P.op("dve", lambda h: h.tensor_scalar_mul(out=hacc[:, c, :], in0=psO[:, 0:256], scalar1=s2c),
                                     reads=[BO, Bs], writes=[Bh[c]])
                            else:
                                P.op("dve", lambda h: h.scalar_tensor_tensor(
                                    out=hacc[:, c, :], in0=psO[:, 0:256], scalar=s2c, in1=hacc[:, c, :], op0=ALU.mult, op1=ALU.add),
                                    reads=[BO, Bs, Bh[c]], writes=[Bh[c]])

                        def run(fn, d, i):
                            if 0 <= i < NCH:
                                fn(d, i)

                        bg_a = {2 + 3 * n_: ld for n_, ld in enumerate(bg_loads)}
                        bg_c = {4 + 3 * n_: (n_, ld) for n_, ld in enumerate(bg_loads)}
                        for t in range(NCH + 3):
                            if t in bg_a:
                                dst_, dcol_, col0_, ncol_, Bd_ = bg_a[t]
                                s_ = ((t - 2) // 3) % 2
                                P.dma("sp", lambda h: h.dma_start(
                                    out=wst[s_][:, :, 0:ncol_],
                                    in_=w_in[:, col0_:col0_ + ncol_].rearrange("(k p) c -> p k c", p=128)),
                                    dwst[s_], writes=[Bwst[s_]])
                            if t in bg_c:
                                n_, (dst_, dcol_, col0_, ncol_, Bd_) = bg_c[t]
                                s_ = n_ % 2
                                copy_op("dve", dst_[:, :, dcol_:dcol_ + ncol_], wst[s_][:, :, 0:ncol_], [Bwst[s_]], [Bd_])
                            run(K2, 0, t - 1)
                            run(K3, 0, t - 1)
                            run(K0, 0, t)
                            run(K0, 1, t)
                            run(K1, 0, t)
                            run(K1, 1, t)
                            run(G, 0, t)
                            run(G, 1, t)
                            run(O2, 0, t - 2)
                            run(K2, 1, t - 1)
                            run(K3, 1, t - 1)
                            run(O2, 1, t - 2)
                            run(O0, 0, t - 1)
                            run(O0, 1, t - 1)
                            run(O1, 0, t - 1)
                            run(O1, 1, t - 1)
                            run(O3, 0, t - 3)
                            run(O3, 1, t - 3)
                    P.barrier()
                    stage(f"scan_{hd}")

                    with ExitStack() as st3:
                        def sb3(name, shape, dt):
                            return st3.enter_context(nc.sbuf_tensor(un(name), list(shape), dt))

                        junk = sb3("junk2", [128, 256], BF16)
                        Bjunk = Buf()
                        hss = sb3("hss", [128, 32], F32)
                        Bhss = Buf()
                        P.op("pool", lambda h: h.memset(hss[:], 0.0), writes=[Bhss])
                        zall = sb3("zall", [128, 32, 256], BF16)
                        Bz = bufs(32)
                        uT = [sb3(f"uT{i}", [128, 2, 512], BF16) for i in range(2)]
                        BuT = bufs(2)
                        Bpu = [BK[0], BK[1]]
                        Bpz = [BK[2], BK[3]]
                        def s2a_u(blk):
                            s = load_aT(blk, 512)
                            us = blk % 2
                            for cc in range(2):
                                pu = banks[cc]
                                for k in range(8):
                                    mm(pu[:, :], wu[:, k, cc * 128:(cc + 1) * 128], aTb[s][:, k, :], k == 0, k == 7, [Bwu, BaTb[s]], [Bpu[cc]])
                                copy_op("dve", uT[us][:, cc, :], pu[:, :], [Bpu[cc]], [BuT[us]])

                        def s2a_z(blk):
                            us = blk % 2
                            for ti in range(4):
                                T = blk * 4 + ti
                                pz = banks[2 + (T % 2)]
                                for cc in range(2):
                                    mm(pz[:, 0:256], uT[us][:, cc, ti * 128:(ti + 1) * 128], wpb[:, cc, :], cc == 0, cc == 1,
                                       [BuT[us], Bwp], [Bpz[T % 2]])
                                copy_op("dve", zall[:, T, :], pz[:, 0:256], [Bpz[T % 2]], [Bz[T]])

                        skewed(8, [(s2a_u, 0), (s2a_z, 1)])

                        for c in range(32):
                            P.op("act", lambda h, c=c: h.activation(out=junk[:], in_=hacc[:, c, :], func=AF.Square,
                                                                  accum_out=hss[:, c:c + 1]), reads=[Bh[c]], writes=[Bjunk, Bhss])
                        P.op("dve", lambda h: h.tensor_scalar(out=hss[:], in0=hss[:], scalar1=1.0 / 256, scalar2=EPS,
                                                             op0=ALU.mult, op1=ALU.add), reads=[Bhss], writes=[Bhss])
                        P.op("act", lambda h: h.activation(out=hss[:], in_=hss[:], func=AF.Sqrt), reads=[Bhss], writes=[Bhss])
                        P.op("dve", lambda h: h.reciprocal(out=hss[:], in_=hss[:]), reads=[Bhss], writes=[Bhss])


                        stage(f"s2a_{hd}")
                        pB = sb3("pB", [128, NBLK, 128], BF16)
                        invc = sb3("invc", [128, 4, 32], F32)
                        BpB = Buf()
                        P.dma("sp", lambda h: h.dma_start(out=pB[:], in_=poolB_in), dsem_for(BpB), writes=[BpB])
                        P.dma("sp", lambda h: h.dma_start(out=invc[:], in_=invc_in), dsem_for(BpB), writes=[BpB])
                        sog = [sb3(f"sog{i}", [128, 768], F32) for i in range(2)]
                        Bsog = bufs(2)
                        p1 = [sb3(f"p1{i}", [128, 256], F32) for i in range(2)]
                        m1 = [sb3(f"m1{i}", [128, 256], F32) for i in range(2)]
                        Bp1, Bm1 = bufs(2), bufs(2)
                        og = [sb3(f"og{i}", [128, 256], F32) for i in range(2)]
                        Bog = bufs(2)
                        ybf = [sb3(f"ybf{i}", [128, 256], BF16) for i in range(3)]
                        Bybf = bufs(3)
                        yst = [sb3(f"yst{i}", [128, 2, 512], BF16) for i in range(2)]
                        Byst = bufs(2)
                        dyst = [nsem(f"dyst{i}") for i in range(2)]
                        BpA = [BK[0], BK[3]]
                        BpG = [BK[1], BK[4]]
                        BpP = BpG
                        BpY = [BK[2], BK[5], BK[6]]
                        pYb = [2, 5, 6]
                        wi_ = hd
                        slot2b = {}

                        def s2b_main(T):
                            blk, ti = T // 4, T % 4
                            if ti == 0:
                                slot2b[blk] = load_aT(blk, 512)
                            s = slot2b[blk]
                            e = T % 2
                            pA = banks[3 * e]
                            pG = banks[3 * e + 1][:, 0:256]
                            pPl = banks[3 * e + 1][:, 256:512]
                            offs = [o for o in range(-HALO[wi_], HALO[wi_] + 1) if (wi_, o) in pidx and 0 <= T + o < 32]
                            for i_, o in enumerate(offs):
                                mm(pPl, pB[:, pidx[(wi_, o)], :], zall[:, T + o, :], i_ == 0, i_ == len(offs) - 1, [BpB, Bz[T + o]], [BpP[e]])
                            for k in range(8):
                                mm(pA[:, :], aTb[s][:, k, ti * 128:(ti + 1) * 128], wogg[:, k, 0:512], k == 0, k == 7, [Bwogg, BaTb[s]], [BpA[e]])
                            for k in range(8):
                                mm(pG, aTb[s][:, k, ti * 128:(ti + 1) * 128], wogg[:, k, 512:768], k == 0, k == 7, [Bwogg, BaTb[s]], [BpG[e]])
                            P.op("act", lambda h: h.activation(out=sog[e][:, 0:512], in_=pA[:, :], func=AF.Sigmoid), reads=[BpA[e]], writes=[Bsog[e]])
                            P.op("act", lambda h: h.activation(out=sog[e][:, 512:768], in_=pG, func=AF.Sigmoid), reads=[BpG[e]], writes=[Bsog[e]])
                            P.op("dve", lambda h: h.scalar_tensor_tensor(out=p1[e][:], in0=pPl, scalar=invc[:, wi_, T:T + 1], in1=zall[:, T, :],
                                                                        op0=ALU.mult, op1=ALU.subtract), reads=[BpP[e], BpB, Bz[T]], writes=[Bp1[e]])
                            P.op("dve", lambda h: h.tensor_tensor(out=og[e][:], in0=sog[e][:, 0:256], in1=sog[e][:, 512:768], op=ALU.mult),
                                 reads=[Bsog[e]], writes=[Bog[e]])
                            P.op("dve", lambda h: h.tensor_tensor(out=p1[e][:], in0=p1[e][:], in1=sog[e][:, 256:512], op=ALU.mult),
                                 reads=[Bp1[e], Bsog[e]], writes=[Bp1[e]])
                            P.op("dve", lambda h: h.scalar_tensor_tensor(out=m1[e][:], in0=hacc[:, T, :], scalar=hss[:, T:T + 1], in1=gain_b[:],
                                                                        op0=ALU.mult, op1=ALU.mult), reads=[Bh[T], Bhss, Bsmall], writes=[Bm1[e]])
                            P.op("dve", lambda h: h.tensor_tensor(out=m1[e][:], in0=m1[e][:], in1=og[e][:], op=ALU.mult),
                                 reads=[Bm1[e], Bog[e]], writes=[Bm1[e]])
                            P.op("dve", lambda h: h.tensor_tensor(out=ybf[T % 3][:], in0=p1[e][:], in1=m1[e][:], op=ALU.add),
                                 reads=[Bp1[e], Bm1[e]], writes=[Bybf[T % 3]])

                        def s2b_out(T):
                            blk, ti = T // 4, T % 4
                            ys = blk % 2
                            e = T % 2
                            r3 = T % 3
                            pY = banks[pYb[r3]][:].bitcast(BF16)
                            for j in range(2):
                                tr(pY[:, j * 128:(j + 1) * 128], ybf[r3][:, j * 128:(j + 1) * 128], ident[:], [Bybf[r3], Bconst], [BpY[r3]])
                            copy_op("act", yst[ys][:, :, ti * 128:(ti + 1) * 128], pY[:, 0:256].rearrange("p (j t) -> p j t", j=2),
                                    [BpY[r3]], [Byst[ys]])
                            if ti == 3:
                                P.dma("sp", lambda h: h.dma_start(
                                    out=yT_scr[blk, :, 2 * hd:2 * hd + 2, :],
                                    in_=yst[ys][:]), dyst[ys], reads=[Byst[ys]], writes=[ByT[blk]])

                        skewed(32, [(s2b_main, 0), (s2b_out, 2)])
                P.barrier()
                stage(f"h_{hd}")
            p1st.close()

            with ExitStack() as st:
                def sb(name, shape, dt):
                    return st.enter_context(nc.sbuf_tensor(un(name), list(shape), dt))

                nfb = sb("nfb", [128, D], F32)
                G2b = sb("G2b", [128, D], F32)
                Brow2 = Buf()
                P.dma("sp", lambda h: h.dma_start(out=G2b[:], in_=rows_scr[0, 3:4, :].partition_broadcast(128)),
                      dsem_for(Brow2), reads=[Brscr], writes=[Brow2])
                P.dma("sp", lambda h: h.dma_start(out=nfb[:], in_=norm_final.partition_broadcast(128)), dsem_for(Brow2), writes=[Brow2])
                wob = sb("wob", [128, 8, D], BF16)
                wfo = sb("wfo", [128, NFF, D], BF16)
                Bwob, Bwfo = Buf(), Buf()
                P.dma("sp", lambda h: h.dma_start(out=wob[:], in_=wo_scr.rearrange("k p c -> p k c")),
                      dsem_for(Bwob), reads=[Bwos], writes=[Bwob])
                for q4 in range(2):
                    P.dma(dmaq(), lambda h, q4=q4: h.dma_start(
                        out=wfo[:, 11 * q4:11 * q4 + 11, :], in_=wfo_scr[11 * q4:11 * q4 + 11].rearrange("k p c -> p k c")),
                        dsem_for(Bwfo), reads=[Bwos], writes=[Bwfo])
                yTb = [sb(f"yTb{i}", [128, 8, 512], BF16) for i in range(2)]
                ByTb = bufs(2)
                dyTb = [P.dsem() for _ in range(2)]
                x1 = [sb(f"x1_{i}", [128, 4, D], F32) for i in range(2)]
                Bx1 = [bufs(4) for _ in range(2)]
                dx1 = [[P.dsem() for _ in range(4)] for _ in range(2)]
                a2T = [sb(f"a2T{i}", [128, 8, 512], BF16) for i in range(2)]
                Ba2T = bufs(2)
                hidT = sb("hidT", [128, NFF, 512], BF16)
                Bhid = bufs(NFF)
                wfi = [sb(f"wfi{i}", [128, 8, 256], BF16) for i in range(3)]
                Bwfi_s = bufs(3)
                dwfi = [P.dsem() for _ in range(3)]
                a2 = [sb(f"a2{i}", [128, D], BF16) for i in range(2)]
                Ba2 = bufs(2)
                sg = [sb(f"sg{i}", [128, 512], F32) for i in range(2)]
                Bsg = bufs(2)
                ost = [sb(f"ost{i}", [128, D], F32) for i in range(2)]
                Bost = bufs(2)
                dost = [P.dsem() for _ in range(2)]
                junk = sb("junk3", [128, D], BF16)
                Bjunk = Buf()
                st2 = sb("st2", [128, 2, 32], F32)
                Bst2 = [bufs(32), bufs(32)]
                P.op("pool", lambda h: h.memset(st2[:], 0.0), writes=Bst2[0] + Bst2[1])
                BpGU = [BK[3], BK[4], BK[5], BK[6]]
                fin = []
                wjc = [0]

                def p2_load(blk):
                    b2 = blk % 2
                    P.dma("sp", lambda h: h.dma_start(out=yTb[b2][:], in_=yT_scr[blk]),
                          dyTb[b2], reads=[ByT[blk]], writes=[ByTb[b2]])
                    for ti in range(4):
                        T = blk * 4 + ti
                        P.dma("sp", lambda h, ti=ti, T=T: h.dma_start(out=x1[b2][:, ti, :], in_=x[T * 128:(T + 1) * 128, :]),
                              dx1[b2][ti], writes=[Bx1[b2][ti]])

                def rms_chain(b2, ti, T, which):
                    col = st2[:, which, T:T + 1]
                    Bs = Bst2[which][T]
                    P.op("act", lambda h: h.activation(out=junk[:], in_=x1[b2][:, ti, :], func=AF.Square, accum_out=col),
                         reads=[Bx1[b2][ti]], writes=[Bjunk, Bs])
                    P.op("act", lambda h: h.activation(out=col, in_=col, func=AF.Sqrt, scale=1.0 / D, bias=epsc[:]),
                         reads=[Bs, Beps], writes=[Bs])
                    P.op("dve", lambda h: h.reciprocal(out=col, in_=col), reads=[Bs], writes=[Bs])
                    return col, Bs

                def p2_op(blk, ti):
                    b2 = blk % 2
                    T = blk * 4 + ti
                    e = T % 2
                    bset = (3, 4) if ti % 2 == 0 else (5, 6)
                    for half in range(2):
                        pM = banks[bset[half]]
                        Bk = BK[bset[half]]
                        for k in range(8):
                            mm(pM[:, :], yTb[b2][:, k, ti * 128:(ti + 1) * 128], wob[:, k, half * 512:(half + 1) * 512], k == 0, k == 7,
                               [ByTb[b2], Bwob], [Bk])
                        P.op("dve", lambda h, half=half, pM=pM: h.tensor_tensor(
                            out=x1[b2][:, ti, half * 512:(half + 1) * 512], in0=pM[:, :], in1=x1[b2][:, ti, half * 512:(half + 1) * 512],
                            op=ALU.add), reads=[Bk, Bx1[b2][ti]], writes=[Bx1[b2][ti]])
                    col, Bs = rms_chain(b2, ti, T, 0)
                    P.op("dve", lambda h: h.scalar_tensor_tensor(out=a2[e][:], in0=x1[b2][:, ti, :], scalar=col, in1=G2b[:],
                                                                op0=ALU.mult, op1=ALU.mult),
                         reads=[Bx1[b2][ti], Bs, Brow2], writes=[Ba2[e]])

                def p2_tr(blk, ti):
                    b2 = blk % 2
                    T = blk * 4 + ti
                    e = T % 2
                    bk = 3 if ti % 2 == 0 else 5
                    pT2 = banks[bk][:].bitcast(BF16)
                    for k in range(8):
                        tr(pT2[:, k * 128:(k + 1) * 128], a2[e][:, k * 128:(k + 1) * 128], ident[:], [Ba2[e], Bconst], [BK[bk]])
                    copy_op("act", a2T[b2][:, :, ti * 128:(ti + 1) * 128], pT2.rearrange("p (k t) -> p k t", k=8), [BK[bk]], [Ba2T[b2]])

                def wfi_load(n):
                    if n >= 8 * NFF:
                        return
                    s_, j = n % 3, n % NFF
                    P.dma("sp", lambda h: h.dma_start(out=wfi[s_][:], in_=wfi_scr[j].rearrange("p (k c) -> p k c", k=8)),
                          dwfi[s_], reads=[Bwfi], writes=[Bwfi_s[s_]])

                def p2_ffn_in(blk):
                    b2 = blk % 2
                    for j in range(NFF):
                        n = blk * NFF + j
                        s_ = n % 3
                        if n == 0:
                            wfi_load(0)
                            wfi_load(1)
                        wfi_load(n + 2)
                        e = j % 2
                        pGm = banks[3 + e]
                        pUm = banks[5 + e]
                        for k in range(8):
                            mm(pGm[:, :], wfi[s_][:, k, 0:128], a2T[b2][:, k, :], k == 0, k == 7, [Bwfi_s[s_], Ba2T[b2]], [BpGU[e]])
                        for k in range(8):
                            mm(pUm[:, :], wfi[s_][:, k, 128:256], a2T[b2][:, k, :], k == 0, k == 7, [Bwfi_s[s_], Ba2T[b2]], [BpGU[2 + e]])
                        P.op("act", lambda h, e=e, j=j, pGm=pGm: h.activation(out=sg[e][:], in_=pGm[:, :], func=AF.Silu,
                                                                             bias=fbias[:, 2 * j:2 * j + 1]),
                             reads=[BpGU[e], Bfb], writes=[Bsg[e]])
                        P.op("dve", lambda h, e=e, j=j, pUm=pUm: h.scalar_tensor_tensor(
                            out=hidT[:, j, :], in0=pUm[:, :], scalar=fbias[:, 2 * j + 1:2 * j + 2], in1=sg[e][:],
                            op0=ALU.add, op1=ALU.mult), reads=[Bsg[e], BpGU[2 + e], Bfb], writes=[Bhid[j]])

                def p2_fo(blk, ti):
                    b2 = blk % 2
                    T = blk * 4 + ti
                    e = T % 2
                    bset = (0, 1) if ti % 2 == 0 else (2, 7)
                    for half in range(2):
                        pF = banks[bset[half]]
                        Bk = BK[bset[half]]
                        for k in range(NFF):
                            mm(pF[:, :], hidT[:, k, ti * 128:(ti + 1) * 128], wfo[:, k, half * 512:(half + 1) * 512], k == 0, k == NFF - 1,
                               [Bhid[k], Bwfo], [Bk])
                        P.op("dve", lambda h, half=half, pF=pF: h.tensor_tensor(
                            out=x1[b2][:, ti, half * 512:(half + 1) * 512], in0=pF[:, :], in1=x1[b2][:, ti, half * 512:(half + 1) * 512],
                            op=ALU.add), reads=[Bk, Bx1[b2][ti]], writes=[Bx1[b2][ti]])
                    col, Bs = rms_chain(b2, ti, T, 1)
                    P.op("dve", lambda h: h.scalar_tensor_tensor(out=ost[e][:], in0=x1[b2][:, ti, :], scalar=col, in1=nfb[:],
                                                                op0=ALU.mult, op1=ALU.mult), reads=[Bx1[b2][ti], Bs, Brow2], writes=[Bost[e]])
                    fin.append(P.dma("sp", lambda h: h.dma_start(out=y_out[T * 128:(T + 1) * 128, :], in_=ost[e][:]),
                                     dost[e], reads=[Bost[e]]))

                p2_load(0)
                skewed(4, [(lambda ti: p2_op(0, ti), 0), (lambda ti: p2_tr(0, ti), 1)])
                for blk in range(8):
                    p2_ffn_in(blk)
                    if blk + 1 < 8:
                        p2_load(blk + 1)
                    for ti in range(4):
                        if blk + 1 < 8:
                            p2_op(blk + 1, ti)
                        p2_fo(blk, ti)
                        if blk + 1 < 8:
                            p2_tr(blk + 1, ti)
                P.wait_all("sp", fin[-2:])

        except _Stop:
            pass
        P.barrier()
        P.emit(nc, gst)
    return nc, P


_CACHE = {}


def host_constants():
    poolB, pidx, inv = pool_constants()
    s = np.arange(128)
    maskf = (s[:, None] <= s[None, :]).astype(np.float32)
    maskb = (s[:, None] >= s[None, :]).astype(np.float32)
    bf = ml_dtypes.bfloat16
    return {
        "ident": np.eye(128, dtype=np.float32).astype(bf),
        "maskf": maskf,
        "maskb": maskb,
        "trif": maskf.astype(bf),
        "trib": maskb.astype(bf),
        "onesm": np.ones((128, 128), np.float32).astype(bf),
        "poolB": poolB,
        "invc": inv,
    }


def make_in_maps(inputs):
    consts = host_constants()
    f = lambda a: np.ascontiguousarray(np.asarray(a, dtype=np.float32))
    shared = {
        "c_ctx": f(inputs["c_ctx"]).reshape(1, D),
        "norm_mix": f(inputs["norm_mix"]).reshape(1, D),
        "norm_ffn": f(inputs["norm_ffn"]).reshape(1, D),
        "norm_final": f(inputs["norm_final"]).reshape(1, D),
        "w_ada": f(inputs["w_ada"]).reshape(D, 6 * D),
        "b_ada": f(inputs["b_ada"]).reshape(1, 6 * D),
        "w_in": f(inputs["w_in"]).reshape(D, 7184),
        "b_gates": f(inputs["b_gates"]).reshape(1, 16),
        "conv_w": f(inputs["conv_w"]).reshape(3, 2048),
        "conv_b": f(inputs["conv_b"]).reshape(1, 2048),
        "w_pool": f(inputs["w_pool"]).reshape(4, 256, 256),
        "pool_scale": f(inputs["pool_scale"]).reshape(1, D),
        "mh_gain": f(inputs["mh_gain"]).reshape(1, D),
        "w_out": f(inputs["w_out"]).reshape(D, D),
        "w_ffn_in": f(inputs["w_ffn_in"]).reshape(D, 2 * DFF),
        "w_ffn_out": f(inputs["w_ffn_out"]).reshape(DFF, D),
    }
    shared.update(consts)
    xs, cs, cx = f(inputs["x"]), f(inputs["c"]), f(inputs["ctx"])
    maps = []
    for b in range(8):
        m = dict(shared)
        m["x"] = xs[b]
        m["ctx"] = cx[b]
        m["c"] = cs[b].reshape(1, D)
        maps.append(m)
    return maps


def kernel(**inputs):
    if "nc" not in _CACHE:
        _CACHE["nc"] = build_program()[0]
    nc = _CACHE["nc"]
    maps = make_in_maps(inputs)
    res = run_bass_kernel_spmd(nc, maps, core_ids=list(range(8)))
    return np.stack([np.asarray(r["y"], dtype=np.float32) for r in res.results], axis=0)
```

```python
import os
import numpy as np
import ml_dtypes
from contextlib import ExitStack

import concourse.bass as bass
import concourse.mybir as mybir
from concourse.bass_utils import run_bass_kernel_spmd

F32 = mybir.dt.float32
BF16 = mybir.dt.bfloat16
ALU = mybir.AluOpType
AF = mybir.ActivationFunctionType

D = 1024
SEQ = 4096
CTX = 256
NT = SEQ + CTX
NCH = NT // 128
DFF = 2816
NFF = DFF // 128
EPS = 1e-6
WINS = (2, 4, 8, 16)
HALO = (1, 1, 2, 4)
PADW = 1 + SEQ + 2 + CTX + 1
ENGS = ("pe", "act", "dve", "pool", "sp")


def skewed(n, stages):
    m = max(sk for _, sk in stages)
    for step in range(n + m):
        for fn, sk in stages:
            T = step - sk
            if 0 <= T < n:
                fn(T)


def pos_of_chunk(c):
    return 1 + 128 * c if c < 32 else (SEQ + 3) + 128 * (c - 32)


class Buf:
    __slots__ = ("w", "r", "excl")

    def __init__(self, excl=False):
        self.w = None
        self.r = {}
        self.excl = excl


def bufs(n):
    return [Buf() for _ in range(n)]


class DSem:
    __slots__ = ("id", "count")

    def __init__(self, i):
        self.id = i
        self.count = 0


class _Rec:
    def __getattr__(self, name):
        return lambda *a, **k: (name, a, k)


_REC = _Rec()


class Plan:
    def __init__(self):
        self.ops = {e: [] for e in ENGS}
        self.waited = {e: {} for e in ENGS}
        self.dsems = []
        self.disabled = False

    def dsem(self):
        d = DSem(len(self.dsems))
        self.dsems.append(d)
        return d

    def _deps(self, eng, reads, writes, skip_key=None):
        deps = {}
        ex = [b for b in reads if b.excl]
        if ex:
            reads = [b for b in reads if not b.excl]
            writes = list(writes) + ex

        def add(d, war=False):
            key, val = d
            if key == ("e", eng) and (eng in ("pe", "sp") or war):
                return
            if key == skip_key:
                return
            if deps.get(key, 0) < val:
                deps[key] = val

        for b in reads:
            if b.w is not None:
                add(b.w)
        for b in writes:
            if b.w is not None:
                add(b.w)
            for k, v in b.r.items():
                add((k, v), war=True)
        waits = []
        wd = self.waited[eng]
        for key, val in deps.items():
            if wd.get(key, 0) >= val:
                continue
            wd[key] = val
            waits.append((key, val))
        return waits

    def _mark(self, me, reads, writes):
        key, val = me
        ex = [b for b in reads if b.excl]
        if ex:
            reads = [b for b in reads if not b.excl]
            writes = list(writes) + ex
        for b in reads:
            if b.r.get(key, 0) < val:
                b.r[key] = val
        for b in writes:
            b.w = me
            b.r = {}

    def op(self, eng, fn, reads=(), writes=()):
        if self.disabled:
            return None
        waits = self._deps(eng, reads, writes)
        idx = len(self.ops[eng]) + 1
        self.ops[eng].append((fn(_REC), waits, None))
        me = (("e", eng), idx)
        self._mark(me, reads, writes)
        return me

    def dma(self, q, fn, ds, reads=(), writes=()):
        if self.disabled:
            return None
        waits = self._deps(q, reads, writes, skip_key=("d", ds.id))
        ds.count += 1
        self.ops[q].append((fn(_REC), waits, ds))
        me = (("d", ds.id), 16 * ds.count)
        self._mark(me, reads, writes)
        return me

    def barrier(self):
        if self.disabled:
            return
        deps = []
        for e in ENGS:
            if e != "sp" and len(self.ops[e]) > 0:
                idx = len(self.ops[e])
                while idx > 0 and (self.ops[e][idx - 1][0] is None or self.ops[e][idx - 1][2] is not None):
                    idx -= 1
                if idx > 0:
                    deps.append((("e", e), idx))
        for d in self.dsems:
            if d.count:
                deps.append((("d", d.id), 16 * d.count))
        for e in ENGS:
            waits = []
            for key, val in deps:
                if key == ("e", e):
                    continue
                if self.waited[e].get(key, 0) < val:
                    self.waited[e][key] = val
                    waits.append((key, val))
            if waits:
                self.ops[e].append((None, waits, None))

    def wait_all(self, eng, deps):
        if self.disabled:
            return
        waits = []
        for key, val in deps:
            if self.waited[eng].get(key, 0) < val:
                self.waited[eng][key] = val
                waits.append((key, val))
        self.ops[eng].append((None, waits, None))

    def emit(self, nc, stack):
        ms = {e: set() for e in ENGS}
        for e in ENGS:
            for fn, waits, ds in self.ops[e]:
                for key, val in waits:
                    if key[0] == "e":
                        ms[key[1]].add(val)
        rank = {e: {idx: i + 1 for i, idx in enumerate(sorted(ms[e]))} for e in ENGS}
        esem = {e: stack.enter_context(nc.semaphore(f"es_{e}")) for e in ENGS}
        dsem = [stack.enter_context(nc.semaphore(f"ds_{d.id}")) for d in self.dsems]
        self.stats = {e: (len(self.ops[e]), len(ms[e])) for e in ENGS}

        def replay(h, e):
            myrank = rank[e]
            for i, (fn, waits, ds) in enumerate(self.ops[e]):
                for key, val in waits:
                    if key[0] == "e":
                        h.wait_ge(esem[key[1]], rank[key[1]][val])
                    else:
                        h.wait_ge(dsem[key[1]], val)
                if fn is None:
                    continue
                ins = getattr(h, fn[0])(*fn[1], **fn[2])
                if ds is not None:
                    ins.then_inc(dsem[ds.id], 16)
                elif (i + 1) in myrank:
                    ins.then_inc(esem[e], 1)

        with nc.Block() as block:
            @block.tensor
            def _(h):
                replay(h, "pe")

            @block.scalar
            def _(h):
                replay(h, "act")

            @block.vector
            def _(h):
                replay(h, "dve")

            @block.gpsimd
            def _(h):
                replay(h, "pool")

            @block.sync
            def _(h):
                replay(h, "sp")


def pool_constants():
    blocks = []
    index = {}
    l = np.arange(128)
    rl, cl = l // 64, l % 64
    for wi, w in enumerate(WINS):
        hw = w // 2
        for o in range(-HALO[wi], HALO[wi] + 1):
            dr = 2 * o + rl[:, None] - rl[None, :]
            dc = cl[:, None] - cl[None, :]
            m = ((dr >= -hw) & (dr <= hw - 1) & (dc >= -hw) & (dc <= hw - 1)).astype(np.float32)
            if m.any():
                index[(wi, o)] = len(blocks)
                blocks.append(m)
    poolB = np.stack(blocks, axis=1)
    inv = np.zeros((128, 4, 32), np.float32)
    for wi, w in enumerate(WINS):
        hw = w // 2
        for T in range(32):
            r = 2 * T + rl
            cr = np.minimum(r + hw, 64) - np.maximum(r - hw, 0)
            cc = np.minimum(cl + hw, 64) - np.maximum(cl - hw, 0)
            inv[:, wi, T] = 1.0 / (cr * cc)
    return poolB.astype(ml_dtypes.bfloat16), index, inv


class _Stop(Exception):
    pass


def build_program(dbg=False, stop=None):
    nc = bass.Bass("TRN2", target_bir_lowering=False)
    P = Plan()

    def din(name, shape, dt=F32):
        return nc.dram_tensor(name, list(shape), dt, kind="ExternalInput").ap()

    x = din("x", [SEQ, D])
    ctx = din("ctx", [CTX, D])
    c_in = din("c", [1, D])
    cctx_in = din("c_ctx", [1, D])
    norm_mix = din("norm_mix", [1, D])
    norm_ffn = din("norm_ffn", [1, D])
    norm_final = din("norm_final", [1, D])
    w_ada = din("w_ada", [D, 6 * D])
    b_ada = din("b_ada", [1, 6 * D])
    w_in = din("w_in", [D, 7184])
    b_gates = din("b_gates", [1, 16])
    conv_w = din("conv_w", [3, 2048])
    conv_b = din("conv_b", [1, 2048])
    w_pool = din("w_pool", [4, 256, 256])
    pool_scale = din("pool_scale", [1, D])
    mh_gain = din("mh_gain", [1, D])
    w_out = din("w_out", [D, D])
    w_ffn_in = din("w_ffn_in", [D, 2 * DFF])
    w_ffn_out = din("w_ffn_out", [DFF, D])
    poolB_np, pidx, _ = pool_constants()
    NBLK = poolB_np.shape[1]
    ident_in = din("ident", [128, 128], BF16)
    maskf_in = din("maskf", [128, 128])
    maskb_in = din("maskb", [128, 128])
    trif_in = din("trif", [128, 128], BF16)
    trib_in = din("trib", [128, 128], BF16)
    ones_in = din("onesm", [128, 128], BF16)
    poolB_in = din("poolB", [128, NBLK, 128], BF16)
    invc_in = din("invc", [128, 4, 32])

    y_out = nc.dram_tensor("y", [SEQ, D], F32, kind="ExternalOutput").ap()
    skind = "ExternalOutput" if dbg else "Internal"
    aT_scr = nc.dram_tensor("aT_scr", [9, 128, 8, 512], BF16, kind=skind).ap()
    yT_scr = nc.dram_tensor("yT_scr", [8, 128, 8, 512], BF16, kind=skind).ap()
    rows_scr = nc.dram_tensor("rows_scr", [2, 6, D], F32, kind=skind).ap()
    wfi_scr = nc.dram_tensor("wfi_scr", [NFF, 128, 8 * 256], BF16, kind="Internal").ap()
    wo_scr = nc.dram_tensor("wo_scr", [8, 128, D], BF16, kind="Internal").ap()
    wfo_scr = nc.dram_tensor("wfo_scr", [NFF, 128, D], BF16, kind="Internal").ap()

    def stage(name):
        if stop == name:
            P.barrier()
            P.disabled = True

    rr = [0]
    uid = [0]

    def un(name):
        uid[0] += 1
        return f"s{uid[0]}_{name}"

    def evac_eng():
        rr[0] ^= 1
        return "act" if rr[0] else "dve"

    def copy_op(eng, out, in_, reads, writes):
        if eng == "act":
            return P.op("act", lambda h: h.activation(out=out, in_=in_, func=AF.Copy), reads, writes)
        return P.op(eng, lambda h: h.tensor_copy(out=out, in_=in_), reads, writes)

    def mm(out, lhsT, rhs, start, stop, reads, writes):
        return P.op("pe", lambda h: h.matmul(out, lhsT=lhsT, rhs=rhs, start=start, stop=stop), reads, writes)

    def tr(out, in_, ident, reads, writes):
        return P.op("pe", lambda h: h.transpose(out=out, in_=in_, identity=ident), reads, writes)

    dq = [0]

    def dmaq():
        dq[0] ^= 1
        return "sp"

    with ExitStack() as gst:
        def gsb(name, shape, dt):
            return gst.enter_context(nc.sbuf_tensor(un(name), list(shape), dt))

        banks = [gst.enter_context(nc.psum_tensor(f"bank{i}", [128, 512], F32)) for i in range(6)]
        bank67 = gst.enter_context(nc.psum_tensor("bank67", [128, 1024], F32))
        banks.append(bank67[:, 0:512])
        banks.append(bank67[:, 512:1024])
        BK = [Buf(excl=True) for _ in range(8)]

        ident = gsb("ident", [128, 128], BF16)
        maskf = gsb("maskf", [128, 128], F32)
        maskb = gsb("maskb", [128, 128], F32)
        trif = gsb("trif", [128, 128], BF16)
        trib = gsb("trib", [128, 128], BF16)
        onesm = gsb("onesm", [128, 128], BF16)
        Bconst = Buf()
        _bsem = {}
        _named = {}

        def nsem(name):
            if name not in _named:
                _named[name] = P.dsem()
            return _named[name]

        def dsem_for(b):
            if id(b) not in _bsem:
                _bsem[id(b)] = (P.dsem(), b)
            return _bsem[id(b)][0]

        for dst, src in ((ident, ident_in), (maskf, maskf_in), (maskb, maskb_in), (trif, trif_in),
                         (trib, trib_in), (onesm, ones_in)):
            P.dma("sp", lambda h, dst=dst, src=src: h.dma_start(out=dst[:], in_=src), dsem_for(Bconst), writes=[Bconst])
        erow16 = gsb("erow16", [128, 2, NCH, 4], F32)
        ecol = gsb("ecol", [128, 2, NCH, 4], F32)
        wsv = gsb("wsv", [128, 2, NCH, 4], F32)
        ebl = gsb("ebl", [128, 2, NCH, 4], F32)
        Btab = Buf()
        epsc = gsb("epsc", [128, 1], F32)
        Beps = Buf()
        P.op("dve", lambda h: h.memset(epsc[:], EPS), writes=[Beps])
        Bwfi = Buf()
        Bwos = Buf()
        G2c = gsb("G2c", [128, 8], F32)
        sh2c = gsb("sh2c", [128, 8], F32)
        sh2cb = gsb("sh2cb", [128, 8], BF16)
        fbias = gsb("fbias", [128, 2 * NFF], F32)
        Bcol = Buf()
        Bfb = Buf()

        try:
            with ExitStack() as st:
                def sb(name, shape, dt):
                    return st.enter_context(nc.sbuf_tensor(un(name), list(shape), dt))

                c2f = sb("c2f", [128, 8, 2], F32)
                c2b = sb("c2b", [128, 8, 2], BF16)
                Bc2 = Buf()
                P.dma("sp", lambda h: h.dma_start(out=c2f[:, :, 0:1], in_=c_in.rearrange("o (k p) -> p k o", p=128),
                                                  allow_slow_non_contiguous=True), dsem_for(Bc2), writes=[Bc2])
                P.dma("sp", lambda h: h.dma_start(out=c2f[:, :, 1:2], in_=cctx_in.rearrange("o (k p) -> p k o", p=128),
                                                  allow_slow_non_contiguous=True), dsem_for(Bc2), writes=[Bc2])
                P.op("act", lambda h: h.activation(out=c2b[:], in_=c2f[:], func=AF.Silu), reads=[Bc2], writes=[Bc2])
                bada = sb("bada", [2, 6 * D], F32)
                nmix = sb("nmix", [2, D], F32)
                nffn = sb("nffn", [2, D], F32)
                Bb = Buf()
                P.dma("sp", lambda h: h.dma_start(out=bada[:], in_=b_ada.partition_broadcast(2)), dsem_for(Bb), writes=[Bb])
                P.dma("sp", lambda h: h.dma_start(out=nmix[:], in_=norm_mix.partition_broadcast(2)), dsem_for(Bb), writes=[Bb])
                P.dma("sp", lambda h: h.dma_start(out=nffn[:], in_=norm_ffn.partition_broadcast(2)), dsem_for(Bb), writes=[Bb])
                modrow = sb("modrow", [2, 6 * D], F32)
                Bmod = Buf()
                wst = [sb(f"wst{i}", [128, 8, 512], F32) for i in range(4)]
                wbf = [sb(f"wbf{i}", [128, 8, 512], BF16) for i in range(4)]
                Bwst, Bwbf = bufs(4), bufs(4)
                dwst = [P.dsem() for _ in range(4)]
                pm = banks[0]
                Bpm = BK[0]
                for blk in range(12):
                    s = blk % 4
                    P.dma(dmaq(), lambda h, s=s, blk=blk: h.dma_start(
                        out=wst[s][:], in_=w_ada[:, blk * 512:(blk + 1) * 512].rearrange("(k p) c -> p k c", p=128)),
                        dwst[s], writes=[Bwst[s]])
                    copy_op(evac_eng(), wbf[s][:], wst[s][:], [Bwst[s]], [Bwbf[s]])
                    for k in range(8):
                        mm(pm[0:2, :], c2b[:, k, :], wbf[s][:, k, :], k == 0, k == 7, [Bc2, Bwbf[s]], [Bpm])
                    P.op("dve", lambda h, blk=blk: h.tensor_tensor(
                        out=modrow[:, blk * 512:(blk + 1) * 512], in0=pm[0:2, :], in1=bada[:, blk * 512:(blk + 1) * 512],
                        op=ALU.add), reads=[Bpm, Bb], writes=[Bmod])
                rows = sb("rows", [2, 6, D], F32)
                Brows = Buf()
                P.op("dve", lambda h: h.scalar_tensor_tensor(out=rows[:, 0, :], in0=modrow[:, D:2 * D], scalar=1.0,
                                                            in1=nmix[:], op0=ALU.add, op1=ALU.mult),
                     reads=[Bmod, Bb], writes=[Brows])
                P.op("dve", lambda h: h.scalar_tensor_tensor(out=rows[:, 3, :], in0=modrow[:, 4 * D:5 * D], scalar=1.0,
                                                            in1=nffn[:], op0=ALU.add, op1=ALU.mult),
                     reads=[Bmod, Bb], writes=[Brows])
                for slot, src in ((1, 0), (2, 2), (4, 3), (5, 5)):
                    P.op("dve", lambda h, slot=slot, src=src: h.tensor_copy(out=rows[:, slot, :],
                                                                              in_=modrow[:, src * D:(src + 1) * D]),
                         reads=[Bmod], writes=[Brows])
                Brscr = Buf()
                drows = P.dsem()
                P.dma("sp", lambda h: h.dma_start(out=rows_scr, in_=rows[:]), drows, reads=[Brows], writes=[Brscr])

            P.dma("sp", lambda h: h.dma_start(out=G2c[:], in_=rows_scr[0, 3, :].rearrange("(k p) -> p k", p=128),
                                              allow_slow_non_contiguous=True), dsem_for(Bcol), reads=[Brscr], writes=[Bcol])
            P.dma("sp", lambda h: h.dma_start(out=sh2c[:], in_=rows_scr[0, 4, :].rearrange("(k p) -> p k", p=128),
                                              allow_slow_non_contiguous=True), dsem_for(Bcol), reads=[Brscr], writes=[Bcol])
            P.op("dve", lambda h: h.tensor_copy(out=sh2cb[:], in_=sh2c[:]), reads=[Bcol], writes=[Bcol])
            P.barrier()
            stage("M")

            p1st = ExitStack()
            wqk2 = [p1st.enter_context(nc.sbuf_tensor(un(f"wqk{i}"), [128, 8, 512], BF16)) for i in range(2)]
            wv2 = [p1st.enter_context(nc.sbuf_tensor(un(f"wv{i}"), [128, 8, 256 + 16], BF16)) for i in range(2)]
            wu = p1st.enter_context(nc.sbuf_tensor(un("wu"), [128, 8, 256], BF16))
            wogg = p1st.enter_context(nc.sbuf_tensor(un("wogg"), [128, 8, 768], BF16))
            Bwqk2, Bwv2 = bufs(2), bufs(2)
            Bwu, Bwogg = Buf(), Buf()
            with ExitStack() as st:
                def sb(name, shape, dt):
                    return st.enter_context(nc.sbuf_tensor(un(name), list(shape), dt))

                cols1 = sb("cols1", [128, 4, 8], F32)
                Brow0 = Buf()
                for ci, (r, slot) in enumerate(((0, 0), (0, 1), (1, 0), (1, 1))):
                    P.dma("sp", lambda h, ci=ci, r=r, slot=slot: h.dma_start(
                        out=cols1[:, ci, :], in_=rows_scr[r, slot, :].rearrange("(k p) -> p k", p=128),
                        allow_slow_non_contiguous=True), dsem_for(Brow0), reads=[Brscr], writes=[Brow0])
                xt = [sb(f"xt{i}", [128, D], F32) for i in range(8)]
                Bxt = bufs(8)
                dxt = [P.dsem() for _ in range(8)]
                junk = sb("junk", [128, D], BF16)
                Bjunk = Buf()
                ssa = sb("ssa", [128, NCH], F32)
                rsa = sb("rsa", [128, NCH], F32)
                Bss = bufs(NCH)
                P.op("pool", lambda h: h.memset(ssa[:], 0.0), writes=Bss)
                abf = [sb(f"abf{i}", [128, D], BF16) for i in range(3)]
                Babf = bufs(3)
                stg = [sb(f"stg{i}", [128, 8, 512], BF16) for i in range(2)]
                Bstg = bufs(2)
                dstg = [P.dsem() for _ in range(2)]
                pT = [banks[i][:].bitcast(BF16) for i in range(3)]
                BpT = [BK[0], BK[1], BK[2]]
                BaT = bufs(9)
                def p0_load(T):
                    s4 = T % 8
                    src = x[T * 128:(T + 1) * 128, :] if T < 32 else ctx[(T - 32) * 128:(T - 31) * 128, :]
                    P.dma(dmaq(), lambda h: h.dma_start(out=xt[s4][:], in_=src), dxt[s4], writes=[Bxt[s4]])

                def p0_stats(T):
                    s4 = T % 8
                    P.op("act", lambda h: h.activation(out=junk[:], in_=xt[s4][:], func=AF.Square, accum_out=ssa[:, T:T + 1]),
                         reads=[Bxt[s4]], writes=[Bjunk, Bss[T]])
                    P.op("act", lambda h: h.activation(out=rsa[:, T:T + 1], in_=ssa[:, T:T + 1], func=AF.Sqrt, scale=1.0 / D,
                                                       bias=epsc[:]), reads=[Bss[T], Beps], writes=[Bss[T]])

                def p0_norm(T):
                    s4, s3 = T % 8, T % 3
                    P.op("dve", lambda h: h.reciprocal(out=rsa[:, T:T + 1], in_=rsa[:, T:T + 1]), reads=[Bss[T]], writes=[Bss[T]])
                    P.op("dve", lambda h: h.tensor_scalar_mul(out=abf[s3][:], in0=xt[s4][:], scalar1=rsa[:, T:T + 1]),
                         reads=[Bxt[s4], Bss[T]], writes=[Babf[s3]])

                def p0_reg(blk, k, t0, t1):
                    bnk = 4 * (blk % 2) + k // 2
                    return banks[bnk][:].bitcast(BF16)[:, (k % 2) * 512 + t0:(k % 2) * 512 + t1], BK[bnk]

                def p0_tr(T):
                    s3 = T % 3
                    blk, ti = T // 4, T % 4
                    for k in range(8):
                        reg, Bk = p0_reg(blk, k, ti * 128, (ti + 1) * 128)
                        tr(reg, abf[s3][:, k * 128:(k + 1) * 128], ident[:], [Babf[s3], Bconst], [Bk])

                def p0_out(T):
                    blk, ti = T // 4, T % 4
                    nt = 512 if blk < 8 else 256
                    if ti != (nt // 128) - 1:
                        return
                    ss_ = blk % 2
                    gi = 0 if blk < 8 else 2
                    for k in range(8):
                        reg, Bk = p0_reg(blk, k, 0, nt)
                        dst = stg[ss_][:, k, 0:nt]
                        if k < 4:
                            P.op("act", lambda h, dst=dst, reg=reg, k=k: h.activation(
                                out=dst, in_=reg, func=AF.Identity, scale=cols1[:, gi, k:k + 1], bias=cols1[:, gi + 1, k:k + 1]),
                                reads=[Bk, Brow0], writes=[Bstg[ss_]])
                        else:
                            P.op("dve", lambda h, dst=dst, reg=reg, k=k: h.tensor_scalar(
                                out=dst, in0=reg, scalar1=cols1[:, gi, k:k + 1], scalar2=cols1[:, gi + 1, k:k + 1],
                                op0=ALU.mult, op1=ALU.add), reads=[Bk, Brow0], writes=[Bstg[ss_]])

                def p0_store(T):
                    blk, ti = T // 4, T % 4
                    nt = 512 if blk < 8 else 256
                    if ti != (nt // 128) - 1:
                        return
                    ss_ = blk % 2
                    P.dma("sp", lambda h: h.dma_start(out=aT_scr[blk, :, :, 0:nt],
                                                      in_=stg[ss_][:, :, 0:nt]), dstg[ss_], reads=[Bstg[ss_]], writes=[BaT[blk]])

                w0st = [sb(f"w0st{i}", [128, 8, 256], F32) for i in range(2)]
                Bw0st = bufs(2)
                dw0 = [P.dsem() for _ in range(2)]
                h0_loads = [(wqk2[0], 0, 1024, 256, Bwqk2[0]), (wqk2[0], 256, 2048, 256, Bwqk2[0]),
                            (wv2[0], 0, 3072, 256, Bwv2[0]), (wv2[0], 256, 7168, 16, Bwv2[0])]

                def p0_w(T):
                    if T in (1, 3, 5, 7):
                        n_ = (T - 1) // 2
                        dst_, dcol_, col0_, ncol_, Bd_ = h0_loads[n_]
                        P.dma("sp", lambda h: h.dma_start(
                            out=w0st[n_ % 2][:, :, 0:ncol_],
                            in_=w_in[:, col0_:col0_ + ncol_].rearrange("(k p) c -> p k c", p=128)),
                            dw0[n_ % 2], writes=[Bw0st[n_ % 2]])
                    if T in (4, 6, 8, 10):
                        n_ = (T - 4) // 2
                        dst_, dcol_, col0_, ncol_, Bd_ = h0_loads[n_]
                        copy_op("dve", dst_[:, :, dcol_:dcol_ + ncol_], w0st[n_ % 2][:, :, 0:ncol_], [Bw0st[n_ % 2]], [Bd_])

                skewed(NCH, [(p0_store, 9), (p0_w, 0), (p0_load, 0), (p0_stats, 4), (p0_norm, 5), (p0_tr, 6), (p0_out, 7)])
            P.barrier()
            stage("0")

            ByT = bufs(8)
            for hd in range(4):
                with ExitStack() as st:
                    def sb(name, shape, dt):
                        return st.enter_context(nc.sbuf_tensor(un(name), list(shape), dt))

                    aTb = [sb(f"aTb{i}", [128, 8, 512], BF16) for i in range(2)]
                    BaTb = bufs(2)
                    daT = [nsem(f"daT{i}") for i in range(2)]
                    wst = [aTb[i][:].bitcast(F32) for i in range(2)]
                    Bwst = BaTb
                    dwst = daT
                    wqk, wv = wqk2[hd % 2], wv2[hd % 2]
                    Bwqk, Bwv = Bwqk2[hd % 2], Bwv2[hd % 2]
                    wi = [0]

                    def load_cols(dst, dcol, col0, ncol, Bdst):
                        s = wi[0] % 2
                        wi[0] += 1
                        P.dma(dmaq(), lambda h: h.dma_start(
                            out=wst[s][:, :, 0:ncol], in_=w_in[:, col0:col0 + ncol].rearrange("(k p) c -> p k c", p=128)),
                            dwst[s], writes=[Bwst[s]])
                        copy_op(evac_eng(), dst[:, :, dcol:dcol + ncol], wst[s][:, :, 0:ncol], [Bwst[s]], [Bdst])

                    nwqk, nwv = wqk2[(hd + 1) % 2], wv2[(hd + 1) % 2]
                    nBwqk, nBwv = Bwqk2[(hd + 1) % 2], Bwv2[(hd + 1) % 2]
                    bg_loads = [(wu, 0, 256 * hd, 256, Bwu), (wogg, 0, 4096 + 256 * hd, 256, Bwogg),
                                (wogg, 256, 5120 + 256 * hd, 256, Bwogg), (wogg, 512, 6144 + 256 * hd, 256, Bwogg)]
                    if hd < 3:
                        bg_loads += [(nwqk, 0, 1024 + 256 * (hd + 1), 256, nBwqk), (nwqk, 256, 2048 + 256 * (hd + 1), 256, nBwqk),
                                     (nwv, 0, 3072 + 256 * (hd + 1), 256, nBwv), (nwv, 256, 7168, 16, nBwv)]
                    stage(f"w1_{hd}")
                    cw = sb("cw", [128, 3, 4], F32)
                    cb = sb("cb", [128, 4], F32)
                    Bcw = Buf()
                    for m in range(4):
                        ch0 = (0 if m < 2 else 1024) + 256 * hd + 128 * (m % 2)
                        P.dma("sp", lambda h, m=m, ch0=ch0: h.dma_start(
                            out=cw[:, :, m:m + 1], in_=conv_w[:, ch0:ch0 + 128].rearrange("j (p o) -> p j o", o=1),
                            allow_slow_non_contiguous=True), dsem_for(Bcw), writes=[Bcw])
                        P.dma("sp", lambda h, m=m, ch0=ch0: h.dma_start(
                            out=cb[:, m:m + 1], in_=conv_b[:, ch0:ch0 + 128].rearrange("o (p q) -> p (o q)", q=1),
                            allow_slow_non_contiguous=True), dsem_for(Bcw), writes=[Bcw])
                    stage(f"w2_{hd}")
                    psc_b = sb("psc_b", [128, 256], F32)
                    gain_b = sb("gain_b", [128, 256], F32)
                    bg_b = sb("bg_b", [128, 16], F32)
                    Bsmall = Buf()
                    P.dma("sp", lambda h: h.dma_start(out=psc_b[:], in_=pool_scale[:, 256 * hd:256 * hd + 256].partition_broadcast(128)),
                          dsem_for(Bsmall), writes=[Bsmall])
                    P.dma("sp", lambda h: h.dma_start(out=gain_b[:], in_=mh_gain[:, 256 * hd:256 * hd + 256].partition_broadcast(128)),
                          dsem_for(Bsmall), writes=[Bsmall])
                    P.dma("sp", lambda h: h.dma_start(out=bg_b[:], in_=b_gates.partition_broadcast(128)), dsem_for(Bsmall), writes=[Bsmall])
                    stage(f"w3_{hd}")
                    wpf = sb("wpf", [128, 2, 256], F32)
                    wpb = sb("wpb", [128, 2, 256], BF16)
                    Bwp = Buf()
                    P.dma("sp", lambda h: h.dma_start(out=wpf[:], in_=w_pool[hd].rearrange("(c p) o -> p c o", p=128)),
                          dsem_for(Bwp), writes=[Bwp])
                    for cc_ in range(2):
                        P.op("dve", lambda h, cc_=cc_: h.tensor_tensor(out=wpb[:, cc_, :], in0=wpf[:, cc_, :], in1=psc_b[:],
                                                                      op=ALU.mult), reads=[Bwp, Bsmall], writes=[Bwp])

                    stage(f"w_{hd}")
                    qkT = sb("qkT", [128, 4, PADW], BF16)
                    BqkT = bufs(4)
                    vaug = sb("vaug", [128, NCH, 257], BF16)
                    Bv = bufs(NCH)
                    Bvone = Buf()
                    P.op("pool", lambda h: h.memset(vaug[:, :, 256:257], 1.0), writes=[Bvone])
                    ai = [0]

                    def load_aT(blk, nt):
                        s = ai[0] % 2
                        ai[0] += 1
                        P.dma(dmaq(), lambda h: h.dma_start(
                            out=aTb[s][:, :, 0:nt], in_=aT_scr[blk, :, :, 0:nt]),
                            daT[s], reads=[BaT[blk]], writes=[BaTb[s]])
                        return s

                    with ExitStack() as st1:
                        pre = st1.enter_context(nc.sbuf_tensor(un("pre"), [128, 4, PADW], BF16))
                        Bpre = bufs(4)
                        gpre = st1.enter_context(nc.sbuf_tensor(un("gpre"), [128, NCH, 16], F32))
                        Bgpre = Buf()
                        for (a0, a1) in ((0, 1), (SEQ + 1, SEQ + 3), (PADW - 1, PADW)):
                            P.op("pool", lambda h, a0=a0, a1=a1: h.memset(pre[:, :, a0:a1], 0.0), writes=Bpre)
                        Bpq = [BK[0], BK[1], BK[2]]
                        Bpv = [BK[3], BK[4]]
                        pqi = [0]
                        dgw = st1.enter_context(nc.sbuf_tensor(un("dgw"), [128, 3, 4, 128], BF16))
                        Bdgw = Buf()
                        for j_ in range(3):
                            for cc_ in range(4):
                                P.op("dve", lambda h, j_=j_, cc_=cc_: h.tensor_scalar_mul(
                                    out=dgw[:, j_, cc_, :], in0=ident[:], scalar1=cw[:, j_, cc_:cc_ + 1]),
                                    reads=[Bconst, Bcw], writes=[Bdgw])
                        cvi = [0]

                        def conv_block(cb_):
                            ntc = 512 if cb_ < 8 else 256
                            pc0 = pos_of_chunk(cb_ * 4)
                            for cc_ in range(4):
                                bnk = 5 + cvi[0] % 3
                                cvi[0] += 1
                                pcv = banks[bnk]
                                for j_ in range(3):
                                    mm(pcv[:, 0:ntc], dgw[:, j_, cc_, :], pre[:, cc_, pc0 + j_ - 1:pc0 + j_ - 1 + ntc],
                                       j_ == 0, j_ == 2, [Bdgw, Bpre[cc_]], [BK[bnk]])
                                P.op("act", lambda h, cc_=cc_, pcv=pcv: h.activation(
                                    out=qkT[:, cc_, pc0:pc0 + ntc], in_=pcv[:, 0:ntc], func=AF.Silu, bias=cb[:, cc_:cc_ + 1]),
                                    reads=[BK[bnk], Bcw], writes=[BqkT[cc_]])

                        NSL = 4
                        wst2 = [st1.enter_context(nc.sbuf_tensor(un(f"wst2_{r}"), [128, 8, 256], F32)) for r in range(2)]
                        wbs = [st1.enter_context(nc.sbuf_tensor(un(f"wbs_{r}"), [128, 8, 256], BF16)) for r in range(NSL)]
                        Bwst2, Bwbs = bufs(2), bufs(NSL)
                        dw2 = [nsem(f"dw2_{i}") for i in range(2)]
                        dw2o = [nsem(f"dw2o_{i}") for i in range(NSL)]
                        gg = [st1.enter_context(nc.sbuf_tensor(un("g1bb"), [128, D], F32)),
                              st1.enter_context(nc.sbuf_tensor(un("g2bb"), [128, D], F32))]
                        Bgg = Buf()
                        for dst, slot in ((gg[0], 2), (gg[1], 5)):
                            P.dma("sp", lambda h, dst=dst, slot=slot: h.dma_start(
                                out=dst[:], in_=rows_scr[0, slot:slot + 1, :].partition_broadcast(128)),
                                dsem_for(Bgg), reads=[Brscr], writes=[Bgg])
                        NPIECE = NFF + 4 + NFF // 2
                        per_head = -(-NPIECE // 4)
                        pieces = list(range(per_head * hd, min(per_head * (hd + 1), NPIECE)))

                        def v2(t):
                            return t[:].rearrange("p k c -> p (k c)").rearrange("p (a b) -> p a b", a=2)

                        def pw_dma(j):
                            s_ = j % 2
                            if j < NFF:
                                P.dma("sp", lambda h: h.dma_start(
                                    out=wst2[s_][:, :, 0:128], in_=w_ffn_in[:, j * 128:(j + 1) * 128].rearrange("(k p) c -> p k c", p=128)),
                                    dw2[s_], writes=[Bwst2[s_]])
                                P.dma("sp", lambda h: h.dma_start(
                                    out=wst2[s_][:, :, 128:256],
                                    in_=w_ffn_in[:, DFF + j * 128:DFF + (j + 1) * 128].rearrange("(k p) c -> p k c", p=128)),
                                    dw2[s_], writes=[Bwst2[s_]])
                            else:
                                src, k2 = (w_out, j - NFF) if j < NFF + 4 else (w_ffn_out, j - NFF - 4)
                                P.dma("sp", lambda h: h.dma_start(
                                    out=v2(wst2[s_]), in_=src[k2 * 256:(k2 + 1) * 256, :].rearrange("(k p) c -> p k c", p=128)),
                                    dw2[s_], writes=[Bwst2[s_]])

                        def pw_cast(j):
                            s_ = j % NSL
                            w_ = j % 2
                            if j < NFF:
                                copy_op("act", wbs[s_][:], wst2[w_][:], [Bwst2[w_]], [Bwbs[s_]])
                            else:
                                gb = gg[0] if j < NFF + 4 else gg[1]
                                P.op("dve", lambda h: h.tensor_tensor(out=v2(wbs[s_]), in0=v2(wst2[w_]),
                                                                      in1=gb[:].unsqueeze(1).to_broadcast([128, 2, D]), op=ALU.mult),
                                     reads=[Bwst2[w_], Bgg], writes=[Bwbs[s_]])

                        def pw_fin(j):
                            s_ = j % NSL
                            if j < NFF:
                                for half in range(2):
                                    col = 300 + 2 * j + half
                                    for k in range(8):
                                        mm(banks[4][:, col:col + 1], wbs[s_][:, k, half * 128:(half + 1) * 128], sh2cb[:, k:k + 1],
                                           k == 0, k == 7, [Bwbs[s_], Bcol], [BK[4]])
                                P.op("dve", lambda h: h.tensor_copy(out=fbias[:, 2 * j:2 * j + 2], in_=banks[4][:, 300 + 2 * j:302 + 2 * j]),
                                     reads=[BK[4]], writes=[Bfb])
                                P.dma("sp", lambda h: h.dma_start(out=wfi_scr[j].rearrange("p (k c) -> p k c", k=8), in_=wbs[s_][:]),
                                      dw2o[s_], reads=[Bwbs[s_]], writes=[Bwfi])
                            elif j < NFF + 4:
                                k2 = j - NFF
                                P.dma("sp", lambda h: h.dma_start(out=wo_scr[2 * k2:2 * k2 + 2].rearrange("k p c -> p k c"), in_=v2(wbs[s_])),
                                      dw2o[s_], reads=[Bwbs[s_]], writes=[Bwos])
                            else:
                                k2 = j - NFF - 4
                                P.dma("sp", lambda h: h.dma_start(out=wfo_scr[2 * k2:2 * k2 + 2].rearrange("k p c -> p k c"), in_=v2(wbs[s_])),
                                      dw2o[s_], reads=[Bwbs[s_]], writes=[Bwos])

                        def pw_stage(blk_):
                            for fn_, lag in ((pw_fin, 2), (pw_cast, 1), (pw_dma, 0)):
                                b_ = blk_ - lag
                                if 0 <= b_ < 5:
                                    for j_ in pieces[2 * b_:2 * b_ + 2]:
                                        fn_(j_)

                        for blk in range(9):
                            nt = 512 if blk < 8 else 256
                            s = load_aT(blk, nt)
                            pw_stage(blk)
                            if blk >= 2:
                                conv_block(blk - 2)
                            p0 = pos_of_chunk(blk * 4)
                            for cc in range(0 if not os.environ.get("SKIP_QK") else 4, 4):
                                b = pqi[0] % 3
                                pqi[0] += 1
                                pq = banks[b]
                                for k in range(8):
                                    mm(pq[:, 0:nt], wqk[:, k, cc * 128:(cc + 1) * 128], aTb[s][:, k, 0:nt], k == 0, k == 7,
                                       [Bwqk, BaTb[s]], [Bpq[b]])
                                copy_op(evac_eng(), pre[:, cc, p0:p0 + nt], pq[:, 0:nt], [Bpq[b]], [Bpre[cc]])
                            for ti in range(nt // 128 if not os.environ.get("SKIP_V") else 0):
                                ch = blk * 4 + ti
                                b = ch % 2
                                pv = banks[3 + b]
                                nco = 272 if (hd == 0 and not os.environ.get('SKIP_G')) else 256
                                for k in range(8):
                                    mm(pv[:, 0:256], aTb[s][:, k, ti * 128:(ti + 1) * 128], wv[:, k, 0:256], k == 0, k == 7,
                                       [Bwv, BaTb[s]], [Bpv[b]])
                                if nco > 256:
                                    for k in range(8):
                                        mm(pv[:, 256:272], aTb[s][:, k, ti * 128:(ti + 1) * 128], wv[:, k, 256:272], k == 0, k == 7,
                                           [Bwv, BaTb[s]], [Bpv[b]])
                                copy_op("act", vaug[:, ch, 0:256], pv[:, 0:256], [Bpv[b]], [Bv[ch]])
                                if hd == 0 and not os.environ.get('SKIP_G') and not os.environ.get('SKIP_GD'):
                                    P.op("dve", lambda h, ch=ch, pv=pv: h.tensor_tensor(
                                        out=gpre[:, ch, :], in0=pv[:, 256:272], in1=bg_b[:], op=ALU.add),
                                        reads=[Bpv[b], Bsmall], writes=[Bgpre])

                        conv_block(7)
                        conv_block(8)
                        stage(f"proj_{hd}")
                        if hd == 0:
                            lf = st1.enter_context(nc.sbuf_tensor(un("lf"), [128, 2, NCH, 4], F32))
                            lr = st1.enter_context(nc.sbuf_tensor(un("lr"), [128, 2, NCH, 4], F32))
                            lhi = st1.enter_context(nc.sbuf_tensor(un("lhi"), [128, 2, NCH, 4], BF16))
                            lmid = st1.enter_context(nc.sbuf_tensor(un("lmid"), [128, 2, NCH, 4], BF16))
                            llo = st1.enter_context(nc.sbuf_tensor(un("llo"), [128, 2, NCH, 4], BF16))
                            gt1 = st1.enter_context(nc.sbuf_tensor(un("gt1"), [128, 2, NCH, 4], F32))
                            gt2 = st1.enter_context(nc.sbuf_tensor(un("gt2"), [128, 2, NCH, 4], F32))
                            Blf, Bl3, Bg1, Bg2 = bufs(4)
                            for d in range(2):
                                P.op("act", lambda h, d=d: h.activation(out=lf[:, d], in_=gpre[:, :, 4 + 8 * d:8 + 8 * d],
                                                                      func=AF.Exp, scale=-1.0), reads=[Bgpre], writes=[Blf])
                            P.op("dve", lambda h: h.tensor_scalar_add(out=lf[:], in0=lf[:], scalar1=1.0), reads=[Blf], writes=[Blf])
                            P.op("act", lambda h: h.activation(out=lf[:], in_=lf[:], func=AF.Ln), reads=[Blf], writes=[Blf])
                            P.op("dve", lambda h: h.tensor_copy(out=lhi[:], in_=lf[:]), reads=[Blf], writes=[Bl3])
                            P.op("dve", lambda h: h.tensor_tensor(out=lr[:], in0=lf[:], in1=lhi[:], op=ALU.subtract),
                                 reads=[Blf, Bl3], writes=[Bg1])
                            P.op("dve", lambda h: h.tensor_copy(out=lmid[:], in_=lr[:]), reads=[Bg1], writes=[Bl3])
                            P.op("dve", lambda h: h.tensor_tensor(out=lr[:], in0=lr[:], in1=lmid[:], op=ALU.subtract),
                                 reads=[Bg1, Bl3], writes=[Bg1])
                            P.op("dve", lambda h: h.tensor_copy(out=llo[:], in_=lr[:]), reads=[Bg1], writes=[Bl3])
                            NG = NCH * 4
                            for d in range(2):
                                tri = trif if d == 0 else trib
                                pc = banks[5 + d][:, 0:NG]
                                ptot = banks[5 + d][:, NG:2 * NG]
                                BpC = BK[5 + d]
                                for i3, part in enumerate((lhi, lmid, llo)):
                                    mm(pc, tri[:], part[:, d].rearrange("p c g -> p (c g)"), i3 == 0, i3 == 2,
                                       [Bconst, Bl3], [BpC])
                                for i3, part in enumerate((lhi, lmid, llo)):
                                    mm(ptot, onesm[:], part[:, d].rearrange("p c g -> p (c g)"), i3 == 0, i3 == 2,
                                       [Bconst, Bl3], [BpC])
                            for d in range(2):
                                pc = banks[5 + d][:, 0:NG].rearrange("p (c g) -> p c g", g=4)
                                ptot = banks[5 + d][:, NG:2 * NG].rearrange("p (c g) -> p c g", g=4)
                                BpC = BK[5 + d]
                                P.op("act", lambda h, d=d, pc=pc: h.activation(out=erow16[:, d], in_=pc, func=AF.Exp, scale=-1.0),
                                     reads=[BpC], writes=[Btab])
                                P.op("dve", lambda h, d=d: h.tensor_scalar_mul(out=erow16[:, d], in0=erow16[:, d], scalar1=1.0 / 16),
                                     reads=[Btab], writes=[Btab])
                                P.op("dve", lambda h, d=d, pc=pc: h.tensor_tensor(out=gt1[:, d], in0=gpre[:, :, 8 * d:8 * d + 4],
                                                                                in1=pc, op=ALU.add),
                                     reads=[Bgpre, BpC], writes=[Bg1])
                                P.op("act", lambda h, d=d: h.activation(out=ecol[:, d], in_=gt1[:, d], func=AF.Exp),
                                     reads=[Bg1], writes=[Btab])
                                P.op("dve", lambda h, d=d, ptot=ptot: h.tensor_tensor(out=gt2[:, d], in0=gt1[:, d], in1=ptot,
                                                                                    op=ALU.subtract),
                                     reads=[Bg1, BpC], writes=[Bg2])
                                P.op("act", lambda h, d=d: h.activation(out=wsv[:, d], in_=gt2[:, d], func=AF.Exp),
                                     reads=[Bg2], writes=[Btab])
                                P.op("act", lambda h, d=d, ptot=ptot: h.activation(out=ebl[:, d], in_=ptot, func=AF.Exp, scale=-1.0),
                                     reads=[BpC], writes=[Btab])

                        stage(f"gates_{hd}")
                    P.barrier()
                    stage(f"s1_{hd}")

                    hacc = sb("hacc", [128, 32, 256], F32)
                    Bh = bufs(32)
                    st2 = ExitStack()
                    if True:
                        def sb2(name, shape, dt):
                            return st2.enter_context(nc.sbuf_tensor(un(name), list(shape), dt))

                        RING = 4
                        Cb = [[sb2(f"Cb{d}_{r}", [128, 2, 257], BF16) for r in range(RING)] for d in range(2)]
                        BCb = [bufs(RING) for _ in range(2)]
                        kp = [[sb2(f"kp{d}_{r}", [128, 256], BF16) for r in range(2)] for d in range(2)]
                        Bkp = [bufs(2) for _ in range(2)]
                        dg = [[sb2(f"dg{d}_{r}", [128, 128], BF16) for r in range(2)] for d in range(2)]
                        Bdg = [bufs(2) for _ in range(2)]
                        PT = [[sb2(f"PT{d}_{r}", [128, 128], BF16) for r in range(2)] for d in range(2)]
                        BPT = [bufs(2) for _ in range(2)]
                        sc1 = sb2("sc1", [128, 2, NCH], F32)
                        sc2 = sb2("sc2", [128, 2, NCH], F32)
                        Bsc = [bufs(NCH) for _ in range(2)]
                        order = [[32, 33] + list(range(32)), [33, 32] + list(range(31, -1, -1))]
                        masks = [maskf, maskb]
                        hwritten = [False] * 32
                        for d in range(2):
                            P.op("pool", lambda h, d=d: h.memset(Cb[d][0][:], 0.0), writes=[BCb[d][0]])

                        def G(d, i):
                            if i >= NCH - 1:
                                return
                            c = order[d][i]
                            P.op("dve", lambda h: h.tensor_scalar_mul(out=dg[d][i % 2][:], in0=ident[:], scalar1=ebl[:, d, c, hd:hd + 1]),
                                 reads=[Bconst, Btab], writes=[Bdg[d][i % 2]])

                        def K0(d, i):
                            if i >= NCH - 1:
                                return
                            p0 = pos_of_chunk(order[d][i])
                            psT = banks[d][:, 0:128].bitcast(BF16)
                            for j in range(2):
                                tr(psT[:, j * 128:(j + 1) * 128], qkT[:, 2 + j, p0:p0 + 128], ident[:], [BqkT[2 + j], Bconst], [BK[d]])

                        def K1(d, i):
                            if i >= NCH - 1:
                                return
                            c = order[d][i]
                            psT = banks[d][:, 0:128].bitcast(BF16)
                            P.op("act", lambda h: h.activation(out=kp[d][i % 2][:], in_=psT[:, 0:256], func=AF.Copy,
                                                               scale=wsv[:, d, c, hd:hd + 1]), reads=[BK[d], Btab], writes=[Bkp[d][i % 2]])

                        def K2(d, i):
                            if i >= NCH - 1:
                                return
                            c = order[d][i]
                            for j in range(2):
                                psU = banks[6 + j]
                                BU = BK[6 + j]
                                mm(psU[:, 0:257], kp[d][i % 2][:, j * 128:(j + 1) * 128], vaug[:, c, :], True, False,
                                   [Bkp[d][i % 2], Bv[c], Bvone], [BU])
                                mm(psU[:, 0:257], dg[d][i % 2][:], Cb[d][i % RING][:, j, :], False, True, [Bdg[d][i % 2], BCb[d][i % RING]], [BU])

                        def K3(d, i):
                            if i >= NCH - 1:
                                return
                            rn = (i + 1) % RING
                            copy_op("act", Cb[d][rn][:], bank67[:].rearrange("p (j c) -> p j c", j=2)[:, :, 0:257],
                                    [BK[6], BK[7]], [BCb[d][rn]])

                        def O0(d, i):
                            c = order[d][i]
                            if c >= 32:
                                return
                            p0 = pos_of_chunk(c)
                            psS = banks[d][:, 128:256]
                            for j in range(2):
                                mm(psS, qkT[:, 2 + j, p0:p0 + 128], qkT[:, j, p0:p0 + 128], j == 0, j == 1, [BqkT[2 + j], BqkT[j]], [BK[d]])

                        def O1(d, i):
                            c = order[d][i]
                            if c >= 32:
                                return
                            psS = banks[d][:, 128:256]
                            P.op("dve", lambda h: h.scalar_tensor_tensor(
                                out=PT[d][i % 2][:], in0=psS, scalar=ecol[:, d, c, hd:hd + 1], in1=masks[d][:], op0=ALU.mult, op1=ALU.mult),
                                reads=[BK[d], Btab, Bconst], writes=[BPT[d][i % 2]])

                        def O2(d, i):
                            c = order[d][i]
                            if c >= 32:
                                return
                            p0 = pos_of_chunk(c)
                            psO = banks[2 + 2 * d + i % 2]
                            BO = BK[2 + 2 * d + i % 2]
                            mm(psO[:, 0:257], PT[d][i % 2][:], vaug[:, c, :], True, False, [BPT[d][i % 2], Bv[c], Bvone], [BO])
                            for j in range(2):
                                mm(psO[:, 0:257], qkT[:, j, p0:p0 + 128], Cb[d][i % RING][:, j, :], False, j == 1,
                                   [BqkT[j], BCb[d][i % RING]], [BO])

                        def O3(d, i):
                            c = order[d][i]
                            if c >= 32:
                                return
                            psO = banks[2 + 2 * d + i % 2]
                            BO = BK[2 + 2 * d + i % 2]
                            e16 = erow16[:, d, c, hd:hd + 1]
                            s1c = sc1[:, d, c:c + 1]
                            s2c = sc2[:, d, c:c + 1]
                            Bs = Bsc[d][c]
                            P.op("act", lambda h: h.activation(out=s1c, in_=psO[:, 256:257], func=AF.Abs, scale=e16),
                                 reads=[BO, Btab], writes=[Bs])
                            P.op("dve", lambda h: h.tensor_scalar_max(out=s1c, in0=s1c, scalar1=1.0), reads=[Bs], writes=[Bs])
                            P.op("dve", lambda h: h.reciprocal(out=s1c, in_=s1c), reads=[Bs], writes=[Bs])
                            P.op("dve", lambda h: h.tensor_tensor(out=s2c, in0=e16, in1=s1c, op=ALU.mult), reads=[Bs, Btab], writes=[Bs])
                            if not hwritten[c]:
                                hwritten[c] = True
                                P.op("dve", lambda h: h.tensor_scalar_mul(out=hacc[:, c, :], in0=psO[:, 0:256], scalar1=s2c),
                                     reads=[BO, Bs], writes=[Bh[c]])
                            else:
                                P.op("dve", lambda h: h.scalar_tensor_tensor(
                                    out=hacc[:, c, :], in0=psO[:, 0:256], scalar=s2c, in1=hacc[:, c, :], op0=ALU.mult, op1=ALU.add),
                                    reads=[BO, Bs, Bh[c]], writes=[Bh[c]])

                        def run(fn, d, i):
                            if 0 <= i < NCH:
                                fn(d, i)

                        bg_a = {2 + 3 * n_: ld for n_, ld in enumerate(bg_loads)}
                        bg_c = {4 + 3 * n_: (n_, ld) for n_, ld in enumerate(bg_loads)}
                        for t in range(NCH + 3):
                            if t in bg_a:
                                dst_, dcol_, col0_, ncol_, Bd_ = bg_a[t]
                                s_ = ((t - 2) // 3) % 2
                                P.dma("sp", lambda h: h.dma_start(
                                    out=wst[s_][:, :, 0:ncol_],
                                    in_=w_in[:, col0_:col0_ + ncol_].rearrange("(k p) c -> p k c", p=128)),
                                    dwst[s_], writes=[Bwst[s_]])
                            if t in bg_c:
                                n_, (dst_, dcol_, col0_, ncol_, Bd_) = bg_c[t]
                                s_ = n_ % 2
                                copy_op("dve", dst_[:, :, dcol_:dcol_ + ncol_], wst[s_][:, :, 0:ncol_], [Bwst[s_]], [Bd_])
                            run(K2, 0, t - 1)
                            run(K3, 0, t - 1)
                            run(K0, 0, t)
                            run(K0, 1, t)
                            run(K1, 0, t)
                            run(K1, 1, t)
                            run(G, 0, t)
                            run(G, 1, t)
                            run(O2, 0, t - 2)
                            run(K2, 1, t - 1)
                            run(K3, 1, t - 1)
                            run(O2, 1, t - 2)
                            run(O0, 0, t - 1)
                            run(O0, 1, t - 1)
                            run(O1, 0, t - 1)
                            run(O1, 1, t - 1)
                            run(O3, 0, t - 3)
                            run(O3, 1, t - 3)
                    stage(f"scan_{hd}")

                    with ExitStack() as st3:
                        def sb3(name, shape, dt):
                            return st3.enter_context(nc.sbuf_tensor(un(name), list(shape), dt))

                        junk = sb3("junk2", [128, 256], BF16)
                        Bjunk = Buf()
                        hss = sb3("hss", [128, 32], F32)
                        Bhss = Buf()
                        P.op("pool", lambda h: h.memset(hss[:], 0.0), writes=[Bhss])
                        zall = sb3("zall", [128, 32, 256], BF16)
                        Bz = bufs(32)
                        uT = [sb3(f"uT{i}", [128, 2, 512], BF16) for i in range(2)]
                        BuT = bufs(2)
                        Bpu = [BK[0], BK[1]]
                        Bpz = [BK[2], BK[3]]
                        def s2a_u(blk):
                            s = load_aT(blk, 512)
                            us = blk % 2
                            for cc in range(2):
                                pu = banks[cc]
                                for k in range(8):
                                    mm(pu[:, :], wu[:, k, cc * 128:(cc + 1) * 128], aTb[s][:, k, :], k == 0, k == 7, [Bwu, BaTb[s]], [Bpu[cc]])
                                copy_op("dve", uT[us][:, cc, :], pu[:, :], [Bpu[cc]], [BuT[us]])

                        def s2a_z(blk):
                            us = blk % 2
                            for ti in range(4):
                                T = blk * 4 + ti
                                pz = banks[2 + (T % 2)]
                                for cc in range(2):
                                    mm(pz[:, 0:256], uT[us][:, cc, ti * 128:(ti + 1) * 128], wpb[:, cc, :], cc == 0, cc == 1,
                                       [BuT[us], Bwp], [Bpz[T % 2]])
                                copy_op("dve", zall[:, T, :], pz[:, 0:256], [Bpz[T % 2]], [Bz[T]])

                        skewed(8, [(s2a_u, 0), (s2a_z, 1)])

                        for c in range(32):
                            P.op("act", lambda h, c=c: h.activation(out=junk[:], in_=hacc[:, c, :], func=AF.Square,
                                                                  accum_out=hss[:, c:c + 1]), reads=[Bh[c]], writes=[Bjunk, Bhss])
                        P.op("dve", lambda h: h.tensor_scalar(out=hss[:], in0=hss[:], scalar1=1.0 / 256, scalar2=EPS,
                                                             op0=ALU.mult, op1=ALU.add), reads=[Bhss], writes=[Bhss])
                        P.op("act", lambda h: h.activation(out=hss[:], in_=hss[:], func=AF.Sqrt), reads=[Bhss], writes=[Bhss])
                        P.op("dve", lambda h: h.reciprocal(out=hss[:], in_=hss[:]), reads=[Bhss], writes=[Bhss])


                        stage(f"s2a_{hd}")
                        pB = sb3("pB", [128, NBLK, 128], BF16)
                        invc = sb3("invc", [128, 4, 32], F32)
                        BpB = Buf()
                        P.dma("sp", lambda h: h.dma_start(out=pB[:], in_=poolB_in), dsem_for(BpB), writes=[BpB])
                        P.dma("sp", lambda h: h.dma_start(out=invc[:], in_=invc_in), dsem_for(BpB), writes=[BpB])
                        sog = [sb3(f"sog{i}", [128, 768], F32) for i in range(2)]
                        Bsog = bufs(2)
                        p1 = [sb3(f"p1{i}", [128, 256], F32) for i in range(2)]
                        m1 = [sb3(f"m1{i}", [128, 256], F32) for i in range(2)]
                        Bp1, Bm1 = bufs(2), bufs(2)
                        og = [sb3(f"og{i}", [128, 256], F32) for i in range(2)]
                        Bog = bufs(2)
                        ybf = [sb3(f"ybf{i}", [128, 256], BF16) for i in range(3)]
                        Bybf = bufs(3)
                        yst = [sb3(f"yst{i}", [128, 2, 512], BF16) for i in range(2)]
                        Byst = bufs(2)
                        dyst = [nsem(f"dyst{i}") for i in range(2)]
                        BpA = [BK[0], BK[3]]
                        BpG = [BK[1], BK[4]]
                        BpP = BpG
                        BpY = [BK[2], BK[5], BK[6]]
                        pYb = [2, 5, 6]
                        wi_ = hd
                        slot2b = {}

                        def s2b_main(T):
                            blk, ti = T // 4, T % 4
                            if ti == 0:
                                slot2b[blk] = load_aT(blk, 512)
                            s = slot2b[blk]
                            e = T % 2
                            pA = banks[3 * e]
                            pG = banks[3 * e + 1][:, 0:256]
                            pPl = banks[3 * e + 1][:, 256:512]
                            offs = [o for o in range(-HALO[wi_], HALO[wi_] + 1) if (wi_, o) in pidx and 0 <= T + o < 32]
                            for i_, o in enumerate(offs):
                                mm(pPl, pB[:, pidx[(wi_, o)], :], zall[:, T + o, :], i_ == 0, i_ == len(offs) - 1, [BpB, Bz[T + o]], [BpP[e]])
                            for k in range(8):
                                mm(pA[:, :], aTb[s][:, k, ti * 128:(ti + 1) * 128], wogg[:, k, 0:512], k == 0, k == 7, [Bwogg, BaTb[s]], [BpA[e]])
                            for k in range(8):
                                mm(pG, aTb[s][:, k, ti * 128:(ti + 1) * 128], wogg[:, k, 512:768], k == 0, k == 7, [Bwogg, BaTb[s]], [BpG[e]])
                            P.op("act", lambda h: h.activation(out=sog[e][:, 0:512], in_=pA[:, :], func=AF.Sigmoid), reads=[BpA[e]], writes=[Bsog[e]])
                            P.op("act", lambda h: h.activation(out=sog[e][:, 512:768], in_=pG, func=AF.Sigmoid), reads=[BpG[e]], writes=[Bsog[e]])
                            P.op("dve", lambda h: h.scalar_tensor_tensor(out=p1[e][:], in0=pPl, scalar=invc[:, wi_, T:T + 1], in1=zall[:, T, :],
                                                                        op0=ALU.mult, op1=ALU.subtract), reads=[BpP[e], BpB, Bz[T]], writes=[Bp1[e]])
                            P.op("dve", lambda h: h.tensor_tensor(out=og[e][:], in0=sog[e][:, 0:256], in1=sog[e][:, 512:768], op=ALU.mult),
                                 reads=[Bsog[e]], writes=[Bog[e]])
                            P.op("dve", lambda h: h.tensor_tensor(out=p1[e][:], in0=p1[e][:], in1=sog[e][:, 256:512], op=ALU.mult),
                                 reads=[Bp1[e], Bsog[e]], writes=[Bp1[e]])
                            P.op("dve", lambda h: h.scalar_tensor_tensor(out=m1[e][:], in0=hacc[:, T, :], scalar=hss[:, T:T + 1], in1=gain_b[:],
                                                                        op0=ALU.mult, op1=ALU.mult), reads=[Bh[T], Bhss, Bsmall], writes=[Bm1[e]])
                            P.op("dve", lambda h: h.tensor_tensor(out=m1[e][:], in0=m1[e][:], in1=og[e][:], op=ALU.mult),
                                 reads=[Bm1[e], Bog[e]], writes=[Bm1[e]])
                            P.op("dve", lambda h: h.tensor_tensor(out=ybf[T % 3][:], in0=p1[e][:], in1=m1[e][:], op=ALU.add),
                                 reads=[Bp1[e], Bm1[e]], writes=[Bybf[T % 3]])

                        def s2b_out(T):
                            blk, ti = T // 4, T % 4
                            ys = blk % 2
                            e = T % 2
                            r3 = T % 3
                            pY = banks[pYb[r3]][:].bitcast(BF16)
                            for j in range(2):
                                tr(pY[:, j * 128:(j + 1) * 128], ybf[r3][:, j * 128:(j + 1) * 128], ident[:], [Bybf[r3], Bconst], [BpY[r3]])
                            copy_op("act", yst[ys][:, :, ti * 128:(ti + 1) * 128], pY[:, 0:256].rearrange("p (j t) -> p j t", j=2),
                                    [BpY[r3]], [Byst[ys]])
                            if ti == 3:
                                P.dma("sp", lambda h: h.dma_start(
                                    out=yT_scr[blk, :, 2 * hd:2 * hd + 2, :],
                                    in_=yst[ys][:]), dyst[ys], reads=[Byst[ys]], writes=[ByT[blk]])

                        skewed(32, [(s2b_main, 0), (s2b_out, 2)])
                    st2.close()
                P.barrier()
                stage(f"h_{hd}")
            p1st.close()

            with ExitStack() as st:
                def sb(name, shape, dt):
                    return st.enter_context(nc.sbuf_tensor(un(name), list(shape), dt))

                nfb = sb("nfb", [128, D], F32)
                G2b = sb("G2b", [128, D], F32)
                Brow2 = Buf()
                P.dma("sp", lambda h: h.dma_start(out=G2b[:], in_=rows_scr[0, 3:4, :].partition_broadcast(128)),
                      dsem_for(Brow2), reads=[Brscr], writes=[Brow2])
                P.dma("sp", lambda h: h.dma_start(out=nfb[:], in_=norm_final.partition_broadcast(128)), dsem_for(Brow2), writes=[Brow2])
                wob = sb("wob", [128, 8, D], BF16)
                wfo = sb("wfo", [128, NFF, D], BF16)
                Bwob, Bwfo = Buf(), Buf()
                P.dma("sp", lambda h: h.dma_start(out=wob[:], in_=wo_scr.rearrange("k p c -> p k c")),
                      dsem_for(Bwob), reads=[Bwos], writes=[Bwob])
                for q4 in range(2):
                    P.dma(dmaq(), lambda h, q4=q4: h.dma_start(
                        out=wfo[:, 11 * q4:11 * q4 + 11, :], in_=wfo_scr[11 * q4:11 * q4 + 11].rearrange("k p c -> p k c")),
                        dsem_for(Bwfo), reads=[Bwos], writes=[Bwfo])
                yTb = [sb(f"yTb{i}", [128, 8, 512], BF16) for i in range(2)]
                ByTb = bufs(2)
                dyTb = [P.dsem() for _ in range(2)]
                x1 = [sb(f"x1_{i}", [128, 4, D], F32) for i in range(2)]
                Bx1 = [bufs(4) for _ in range(2)]
                dx1 = [[P.dsem() for _ in range(4)] for _ in range(2)]
                a2T = [sb(f"a2T{i}", [128, 8, 512], BF16) for i in range(2)]
                Ba2T = bufs(2)
                hidT = sb("hidT", [128, NFF, 512], BF16)
                Bhid = bufs(NFF)
                wfi = [sb(f"wfi{i}", [128, 8, 256], BF16) for i in range(3)]
                Bwfi_s = bufs(3)
                dwfi = [P.dsem() for _ in range(3)]
                a2 = [sb(f"a2{i}", [128, D], BF16) for i in range(2)]
                Ba2 = bufs(2)
                sg = [sb(f"sg{i}", [128, 512], F32) for i in range(2)]
                Bsg = bufs(2)
                ost = [sb(f"ost{i}", [128, D], F32) for i in range(2)]
                Bost = bufs(2)
                dost = [P.dsem() for _ in range(2)]
                junk = sb("junk3", [128, D], BF16)
                Bjunk = Buf()
                st2 = sb("st2", [128, 2, 32], F32)
                Bst2 = [bufs(32), bufs(32)]
                P.op("pool", lambda h: h.memset(st2[:], 0.0), writes=Bst2[0] + Bst2[1])
                BpGU = [BK[3], BK[4], BK[5], BK[6]]
                fin = []
                wjc = [0]

                def p2_load(blk):
                    b2 = blk % 2
                    P.dma("sp", lambda h: h.dma_start(out=yTb[b2][:], in_=yT_scr[blk]),
                          dyTb[b2], reads=[ByT[blk]], writes=[ByTb[b2]])
                    for ti in range(4):
                        T = blk * 4 + ti
                        P.dma("sp", lambda h, ti=ti, T=T: h.dma_start(out=x1[b2][:, ti, :], in_=x[T * 128:(T + 1) * 128, :]),
                              dx1[b2][ti], writes=[Bx1[b2][ti]])

                def rms_chain(b2, ti, T, which):
                    col = st2[:, which, T:T + 1]
                    Bs = Bst2[which][T]
                    P.op("act", lambda h: h.activation(out=junk[:], in_=x1[b2][:, ti, :], func=AF.Square, accum_out=col),
                         reads=[Bx1[b2][ti]], writes=[Bjunk, Bs])
                    P.op("act", lambda h: h.activation(out=col, in_=col, func=AF.Sqrt, scale=1.0 / D, bias=epsc[:]),
                         reads=[Bs, Beps], writes=[Bs])
                    P.op("dve", lambda h: h.reciprocal(out=col, in_=col), reads=[Bs], writes=[Bs])
                    return col, Bs

                def p2_op(blk, ti):
                    b2 = blk % 2
                    T = blk * 4 + ti
                    e = T % 2
                    bset = (3, 4) if ti % 2 == 0 else (5, 6)
                    for half in range(2):
                        pM = banks[bset[half]]
                        Bk = BK[bset[half]]
                        for k in range(8):
                            mm(pM[:, :], yTb[b2][:, k, ti * 128:(ti + 1) * 128], wob[:, k, half * 512:(half + 1) * 512], k == 0, k == 7,
                               [ByTb[b2], Bwob], [Bk])
                        P.op("dve", lambda h, half=half, pM=pM: h.tensor_tensor(
                            out=x1[b2][:, ti, half * 512:(half + 1) * 512], in0=pM[:, :], in1=x1[b2][:, ti, half * 512:(half + 1) * 512],
                            op=ALU.add), reads=[Bk, Bx1[b2][ti]], writes=[Bx1[b2][ti]])
                    col, Bs = rms_chain(b2, ti, T, 0)
                    P.op("dve", lambda h: h.scalar_tensor_tensor(out=a2[e][:], in0=x1[b2][:, ti, :], scalar=col, in1=G2b[:],
                                                                op0=ALU.mult, op1=ALU.mult),
                         reads=[Bx1[b2][ti], Bs, Brow2], writes=[Ba2[e]])

                def p2_tr(blk, ti):
                    b2 = blk % 2
                    T = blk * 4 + ti
                    e = T % 2
                    bk = 3 if ti % 2 == 0 else 5
                    pT2 = banks[bk][:].bitcast(BF16)
                    for k in range(8):
                        tr(pT2[:, k * 128:(k + 1) * 128], a2[e][:, k * 128:(k + 1) * 128], ident[:], [Ba2[e], Bconst], [BK[bk]])
                    copy_op("act", a2T[b2][:, :, ti * 128:(ti + 1) * 128], pT2.rearrange("p (k t) -> p k t", k=8), [BK[bk]], [Ba2T[b2]])

                def wfi_load(n):
                    if n >= 8 * NFF:
                        return
                    s_, j = n % 3, n % NFF
                    P.dma("sp", lambda h: h.dma_start(out=wfi[s_][:], in_=wfi_scr[j].rearrange("p (k c) -> p k c", k=8)),
                          dwfi[s_], reads=[Bwfi], writes=[Bwfi_s[s_]])

                def p2_ffn_in(blk):
                    b2 = blk % 2
                    for j in range(NFF):
                        n = blk * NFF + j
                        s_ = n % 3
                        if n == 0:
                            wfi_load(0)
                            wfi_load(1)
                        wfi_load(n + 2)
                        e = j % 2
                        pGm = banks[3 + e]
                        pUm = banks[5 + e]
                        for k in range(8):
                            mm(pGm[:, :], wfi[s_][:, k, 0:128], a2T[b2][:, k, :], k == 0, k == 7, [Bwfi_s[s_], Ba2T[b2]], [BpGU[e]])
                        for k in range(8):
                            mm(pUm[:, :], wfi[s_][:, k, 128:256], a2T[b2][:, k, :], k == 0, k == 7, [Bwfi_s[s_], Ba2T[b2]], [BpGU[2 + e]])
                        P.op("act", lambda h, e=e, j=j, pGm=pGm: h.activation(out=sg[e][:], in_=pGm[:, :], func=AF.Silu,
                                                                             bias=fbias[:, 2 * j:2 * j + 1]),
                             reads=[BpGU[e], Bfb], writes=[Bsg[e]])
                        P.op("dve", lambda h, e=e, j=j, pUm=pUm: h.scalar_tensor_tensor(
                            out=hidT[:, j, :], in0=pUm[:, :], scalar=fbias[:, 2 * j + 1:2 * j + 2], in1=sg[e][:],
                            op0=ALU.add, op1=ALU.mult), reads=[Bsg[e], BpGU[2 + e], Bfb], writes=[Bhid[j]])

                def p2_fo(blk, ti):
                    b2 = blk % 2
                    T = blk * 4 + ti
                    e = T % 2
                    bset = (0, 1) if ti % 2 == 0 else (2, 7)
                    for half in range(2):
                        pF = banks[bset[half]]
                        Bk = BK[bset[half]]
                        for k in range(NFF):
                            mm(pF[:, :], hidT[:, k, ti * 128:(ti + 1) * 128], wfo[:, k, half * 512:(half + 1) * 512], k == 0, k == NFF - 1,
                               [Bhid[k], Bwfo], [Bk])
                        P.op("dve", lambda h, half=half, pF=pF: h.tensor_tensor(
                            out=x1[b2][:, ti, half * 512:(half + 1) * 512], in0=pF[:, :], in1=x1[b2][:, ti, half * 512:(half + 1) * 512],
                            op=ALU.add), reads=[Bk, Bx1[b2][ti]], writes=[Bx1[b2][ti]])
                    col, Bs = rms_chain(b2, ti, T, 1)
                    P.op("dve", lambda h: h.scalar_tensor_tensor(out=ost[e][:], in0=x1[b2][:, ti, :], scalar=col, in1=nfb[:],
                                                                op0=ALU.mult, op1=ALU.mult), reads=[Bx1[b2][ti], Bs, Brow2], writes=[Bost[e]])
                    fin.append(P.dma("sp", lambda h: h.dma_start(out=y_out[T * 128:(T + 1) * 128, :], in_=ost[e][:]),
                                     dost[e], reads=[Bost[e]]))

                p2_load(0)
                skewed(4, [(lambda ti: p2_op(0, ti), 0), (lambda ti: p2_tr(0, ti), 1)])
                for blk in range(8):
                    p2_ffn_in(blk)
                    if blk + 1 < 8:
                        p2_load(blk + 1)
                    for ti in range(4):
                        if blk + 1 < 8:
                            p2_op(blk + 1, ti)
                        p2_fo(blk, ti)
                        if blk + 1 < 8:
                            p2_tr(blk + 1, ti)
                P.wait_all("sp", fin[-2:])

        except _Stop:
            pass
        P.barrier()
        P.emit(nc, gst)
    return nc, P


_CACHE = {}


def host_constants():
    poolB, pidx, inv = pool_constants()
    s = np.arange(128)
    maskf = (s[:, None] <= s[None, :]).astype(np.float32)
    maskb = (s[:, None] >= s[None, :]).astype(np.float32)
    bf = ml_dtypes.bfloat16
    return {
        "ident": np.eye(128, dtype=np.float32).astype(bf),
        "maskf": maskf,
        "maskb": maskb,
        "trif": maskf.astype(bf),
        "trib": maskb.astype(bf),
        "onesm": np.ones((128, 128), np.float32).astype(bf),
        "poolB": poolB,
        "invc": inv,
    }


def make_in_maps(inputs):
    consts = host_constants()
    f = lambda a: np.ascontiguousarray(np.asarray(a, dtype=np.float32))
    shared = {
        "c_ctx": f(inputs["c_ctx"]).reshape(1, D),
        "norm_mix": f(inputs["norm_mix"]).reshape(1, D),
        "norm_ffn": f(inputs["norm_ffn"]).reshape(1, D),
        "norm_final": f(inputs["norm_final"]).reshape(1, D),
        "w_ada": f(inputs["w_ada"]).reshape(D, 6 * D),
        "b_ada": f(inputs["b_ada"]).reshape(1, 6 * D),
        "w_in": f(inputs["w_in"]).reshape(D, 7184),
        "b_gates": f(inputs["b_gates"]).reshape(1, 16),
        "conv_w": f(inputs["conv_w"]).reshape(3, 2048),
        "conv_b": f(inputs["conv_b"]).reshape(1, 2048),
        "w_pool": f(inputs["w_pool"]).reshape(4, 256, 256),
        "pool_scale": f(inputs["pool_scale"]).reshape(1, D),
        "mh_gain": f(inputs["mh_gain"]).reshape(1, D),
        "w_out": f(inputs["w_out"]).reshape(D, D),
        "w_ffn_in": f(inputs["w_ffn_in"]).reshape(D, 2 * DFF),
        "w_ffn_out": f(inputs["w_ffn_out"]).reshape(DFF, D),
    }
    shared.update(consts)
    xs, cs, cx = f(inputs["x"]), f(inputs["c"]), f(inputs["ctx"])
    maps = []
    for b in range(8):
        m = dict(shared)
        m["x"] = xs[b]
        m["ctx"] = cx[b]
        m["c"] = cs[b].reshape(1, D)
        maps.append(m)
    return maps


def kernel(**inputs):
    if "nc" not in _CACHE:
        _CACHE["nc"] = build_program()[0]
    nc = _CACHE["nc"]
    maps = make_in_maps(inputs)
    res = run_bass_kernel_spmd(nc, maps, core_ids=list(range(8)))
    return np.stack([np.asarray(r["y"], dtype=np.float32) for r in res.results], axis=0)
```
